# Optimizing a Trainium2 kernel written in Bass

```python
import math
import jax, jax.numpy as jnp
from jax import lax
import numpy as np

D_MODEL = 2048
BATCH = 2
SEQ = 16384
DEPTH = 1

CHUNK = 64
N_META = 16
SSM_WIDTH = D_MODEL // 2
SSM_GROUP = 16
SSM_GROUPS = SSM_WIDTH // SSM_GROUP
SSM_STATE = 64
SB_WIDTH = D_MODEL // 2
SB_HEADS = 8
SB_HEAD_DIM = SB_WIDTH // SB_HEADS
Q_BLOCK = 128
IN_WIDTH = 2 * SSM_WIDTH + 4 * SB_WIDTH
DEEPNORM_ALPHA = (2.0 * DEPTH) ** 0.25
DEEPNORM_BETA = (8.0 * DEPTH) ** -0.25
LN_EPS = 1e-5
DT_MIN = 1e-3
DT_MAX = 1e-1

kernel_name = "hybrid_s5_stickbreaking_gated_deepnorm"


def layer_norm(x, gain, bias):
    xf = x.astype(jnp.float32)
    mu = jnp.mean(xf, axis=-1, keepdims=True)
    var = jnp.mean(jnp.square(xf - mu), axis=-1, keepdims=True)
    return (xf - mu) * lax.rsqrt(var + LN_EPS) * gain.astype(jnp.float32) + bias.astype(jnp.float32)


def _cmul(ar, ai, br, bi):
    return ar * br - ai * bi, ar * bi + ai * br


def s5_branch(u, lam_re, lam_im, log_dt, b_re, b_im, c_re, c_im, d_skip, w_glu, b_glu):
    bsz, L, _ = u.shape
    uf = u.astype(jnp.float32).reshape(bsz, L, SSM_GROUPS, SSM_GROUP)
    lr = lam_re.astype(jnp.float32)
    li = lam_im.astype(jnp.float32)
    dt = jnp.exp(log_dt.astype(jnp.float32))[:, None]
    mag = jnp.exp(lr * dt)
    ab_r = mag * jnp.cos(li * dt)
    ab_i = mag * jnp.sin(li * dt)
    den = lr * lr + li * li
    nr = ab_r - 1.0
    f_r = (nr * lr + ab_i * li) / den
    f_i = (ab_i * lr - nr * li) / den
    br = b_re.astype(jnp.float32)
    bi = b_im.astype(jnp.float32)
    bb_r = f_r[..., None] * br - f_i[..., None] * bi
    bb_i = f_r[..., None] * bi + f_i[..., None] * br
    bu_r = jnp.einsum('gpc,blgc->blgp', bb_r, uf)
    bu_i = jnp.einsum('gpc,blgc->blgp', bb_i, uf)
    a_r = jnp.broadcast_to(ab_r[None, None], (1, L, SSM_GROUPS, SSM_STATE))
    a_i = jnp.broadcast_to(ab_i[None, None], (1, L, SSM_GROUPS, SSM_STATE))

    def combine(e1, e2):
        a1r, a1i, b1r, b1i = e1
        a2r, a2i, b2r, b2i = e2
        ar, ai = _cmul(a2r, a2i, a1r, a1i)
        sr, si = _cmul(a2r, a2i, b1r, b1i)
        return ar, ai, sr + b2r, si + b2i

    _, _, h_r, h_i = lax.associative_scan(combine, (a_r, a_i, bu_r, bu_i), axis=1)
    y = (jnp.einsum('gcp,blgp->blgc', c_re.astype(jnp.float32), h_r)
         - jnp.einsum('gcp,blgp->blgc', c_im.astype(jnp.float32), h_i)
         + d_skip.astype(jnp.float32) * uf)
    y = y.reshape(bsz, L, SSM_WIDTH)
    z = jax.nn.gelu(y)
    hg = z @ w_glu.astype(jnp.float32) + b_glu.astype(jnp.float32)
    return hg[..., :SSM_WIDTH] * jax.nn.sigmoid(hg[..., SSM_WIDTH:])


def stick_breaking_attention(q, k, v):
    bsz, nh, L, d = q.shape
    n_blk = -(-L // Q_BLOCK)
    L_pad = n_blk * Q_BLOCK
    q = jnp.pad(q, ((0, 0), (0, 0), (0, L_pad - L), (0, 0)))
    q_blocks = q.reshape(bsz, nh, n_blk, Q_BLOCK, d).transpose(2, 0, 1, 3, 4)
    starts = jnp.arange(n_blk, dtype=jnp.int32) * Q_BLOCK
    k_pos = jnp.arange(L, dtype=jnp.int32)
    scale = 1.0 / math.sqrt(d)

    def block(args):
        qb, start = args
        z = jnp.einsum('bhqd,bhkd->bhqk', qb, k) * scale
        q_pos = start + jnp.arange(Q_BLOCK, dtype=jnp.int32)
        mask = k_pos[None, :] < q_pos[:, None]
        log_keep = jnp.where(mask, jax.nn.log_sigmoid(-z), 0.0)
        suffix = lax.cumsum(log_keep, axis=3, reverse=True) - log_keep
        a = jnp.where(mask, jnp.exp(jax.nn.log_sigmoid(z) + suffix), 0.0)
        return jnp.einsum('bhqk,bhkd->bhqd', a, v)

    o = lax.map(block, (q_blocks, starts))
    o = o.transpose(1, 2, 0, 3, 4).reshape(bsz, nh, L_pad, d)
    return o[:, :, :L]


def setup_inputs(seed: int = 0) -> dict:
    key = jax.random.key(seed)
    ks = jax.random.split(key, 20)
    f32 = jnp.float32
    n = jnp.arange(SSM_STATE, dtype=f32)
    lam_re = -0.5 + 1e-3 * jax.random.normal(ks[2], (DEPTH, SSM_GROUPS, SSM_STATE), f32)
    lam_im = math.pi * n + 1e-3 * jax.random.normal(ks[3], (DEPTH, SSM_GROUPS, SSM_STATE), f32)
    log_dt = jax.random.uniform(ks[4], (DEPTH, SSM_GROUPS), f32,
                                minval=math.log(DT_MIN), maxval=math.log(DT_MAX))
    return {
        "x": jax.random.normal(ks[0], (BATCH, SEQ, D_MODEL), f32),
        "meta_tokens": jax.random.normal(ks[1], (N_META, D_MODEL), f32),
        "w_in": jax.random.normal(ks[5], (DEPTH, D_MODEL, IN_WIDTH), f32) * D_MODEL ** -0.5,
        "ssm_lambda_re": lam_re,
        "ssm_lambda_im": lam_im,
        "ssm_log_dt": log_dt,
        "ssm_b_re": jax.random.normal(ks[6], (DEPTH, SSM_GROUPS, SSM_STATE, SSM_GROUP), f32) * (2 * SSM_GROUP) ** -0.5,
        "ssm_b_im": jax.random.normal(ks[7], (DEPTH, SSM_GROUPS, SSM_STATE, SSM_GROUP), f32) * (2 * SSM_GROUP) ** -0.5,
        "ssm_c_re": jax.random.normal(ks[8], (DEPTH, SSM_GROUPS, SSM_GROUP, SSM_STATE), f32) * (2 * SSM_STATE) ** -0.5,
        "ssm_c_im": jax.random.normal(ks[9], (DEPTH, SSM_GROUPS, SSM_GROUP, SSM_STATE), f32) * (2 * SSM_STATE) ** -0.5,
        "ssm_d": jax.random.normal(ks[10], (DEPTH, SSM_GROUPS, SSM_GROUP), f32),
        "w_glu": jax.random.normal(ks[11], (DEPTH, SSM_WIDTH, 2 * SSM_WIDTH), f32) * SSM_WIDTH ** -0.5,
        "b_glu": 0.01 * jax.random.normal(ks[12], (DEPTH, 2 * SSM_WIDTH), f32),
        "w_branch_ssm": jax.random.normal(ks[13], (DEPTH, SSM_WIDTH, D_MODEL), f32) * SSM_WIDTH ** -0.5 * DEEPNORM_BETA,
        "w_branch_sb": jax.random.normal(ks[14], (DEPTH, SB_WIDTH, D_MODEL), f32) * SB_WIDTH ** -0.5 * DEEPNORM_BETA,
        "w_gate": jax.random.normal(ks[15], (DEPTH, D_MODEL, 2 * D_MODEL), f32) * D_MODEL ** -0.5,
        "b_gate": 0.01 * jax.random.normal(ks[16], (DEPTH, 2 * D_MODEL), f32),
        "w_out": jax.random.normal(ks[17], (DEPTH, D_MODEL, D_MODEL), f32) * D_MODEL ** -0.5 * DEEPNORM_BETA,
        "ln_gain": 1.0 + 0.02 * jax.random.normal(ks[18], (DEPTH, D_MODEL), f32),
        "ln_bias": 0.02 * jax.random.normal(ks[19], (DEPTH, D_MODEL), f32),
    }


def reference(x, meta_tokens, w_in, ssm_lambda_re, ssm_lambda_im, ssm_log_dt,
              ssm_b_re, ssm_b_im, ssm_c_re, ssm_c_im, ssm_d, w_glu, b_glu,
              w_branch_ssm, w_branch_sb, w_gate, b_gate, w_out, ln_gain, ln_bias):
    bsz = x.shape[0]
    meta = jnp.broadcast_to(meta_tokens[None].astype(x.dtype), (bsz, N_META, D_MODEL))
    h = jnp.concatenate([meta, x], axis=1)
    L = h.shape[1]
    split_at = [SSM_WIDTH, 2 * SSM_WIDTH, 2 * SSM_WIDTH + SB_WIDTH,
                2 * SSM_WIDTH + 2 * SB_WIDTH, 2 * SSM_WIDTH + 3 * SB_WIDTH]
    for layer in range(DEPTH):
        hf = h.astype(jnp.float32)
        proj = hf @ w_in[layer].astype(jnp.float32)
        u_ssm, g_ssm, q, k, v, g_sb = jnp.split(proj, split_at, axis=-1)

        y_ssm = s5_branch(u_ssm, ssm_lambda_re[layer], ssm_lambda_im[layer], ssm_log_dt[layer],
                          ssm_b_re[layer], ssm_b_im[layer], ssm_c_re[layer], ssm_c_im[layer],
                          ssm_d[layer].reshape(SSM_GROUPS, SSM_GROUP), w_glu[layer], b_glu[layer])
        y_ssm = (y_ssm * jax.nn.silu(g_ssm)) @ w_branch_ssm[layer].astype(jnp.float32)

        def heads(t):
            return t.reshape(bsz, L, SB_HEADS, SB_HEAD_DIM).transpose(0, 2, 1, 3)
        o = stick_breaking_attention(heads(q), heads(k), heads(v))
        y_sb = o.transpose(0, 2, 1, 3).reshape(bsz, L, SB_WIDTH)
        y_sb = (y_sb * jax.nn.silu(g_sb)) @ w_branch_sb[layer].astype(jnp.float32)

        gates = jax.nn.sigmoid(hf @ w_gate[layer].astype(jnp.float32) + b_gate[layer].astype(jnp.float32))
        mixed = gates[..., :D_MODEL] * y_ssm + gates[..., D_MODEL:] * y_sb
        sub = mixed @ w_out[layer].astype(jnp.float32)

        h = layer_norm(DEEPNORM_ALPHA * hf + sub, ln_gain[layer], ln_bias[layer]).astype(x.dtype)
    return h[:, N_META:]
```

```python
import math
from contextlib import ExitStack

import numpy as np
import concourse.bass as bass
import concourse.mybir as mybir
from concourse.bass_utils import run_bass_kernel_spmd

ACT = mybir.ActivationFunctionType
ALU = mybir.AluOpType
F32 = mybir.dt.float32
BF16 = mybir.dt.bfloat16
I32 = mybir.dt.int32

D = 2048
NKC = 16
PI = math.pi


class View:
    def __init__(self, ap, buf):
        self.ap = ap
        self.buf = buf


class Buf:
    def __init__(self, t, name):
        self.t = t
        self.name = name
        self.w = None
        self.r = {}
        self.dsem = None
        self.dcnt = 0

    def __getitem__(self, idx):
        return View(self.t[idx], self)

    def v(self, ap):
        return View(ap, self)


class Sched:
    ENGS = ("pe", "act", "dve", "pool", "sp")

    def __init__(self, nc, es):
        self.nc = nc
        self.es = es
        self.es_cur = es
        self.sem = {e: es.enter_context(nc.semaphore("s_" + e)) for e in self.ENGS}
        self.cnt = {e: 0 for e in self.ENGS}
        self.seen = {e: {} for e in self.ENGS}
        self.prog = {e: [] for e in self.ENGS}
        self.nbuf = 0

    def sb(self, shape, dt, name=None):
        self.nbuf += 1
        name = (name or "sb") + f"_{self.nbuf}"
        t = self.es_cur.enter_context(self.nc.sbuf_tensor(name, list(shape), dt))
        return Buf(t, name)

    def ps(self, shape, dt=F32, name=None):
        self.nbuf += 1
        name = (name or "ps") + f"_{self.nbuf}"
        t = self.es_cur.enter_context(self.nc.psum_tensor(name, list(shape), dt))
        return Buf(t, name)

    def dma_sem(self, buf):
        if buf.dsem is None:
            buf.dsem = self.es.enter_context(self.nc.semaphore("d_" + buf.name))
        return buf.dsem

    def _waits(self, eng, reads, writes):
        ev = []
        for b in reads:
            if b.w is not None:
                ev.append(b.w)
        for b in writes:
            if b.w is not None:
                ev.append(b.w)
            ev.extend(b.r.values())
        best = {}
        for (sem, val, src) in ev:
            if src == "pe" and eng == "pe":
                continue
            k = id(sem)
            if self.seen[eng].get(k, 0) >= val:
                continue
            if k not in best or best[k][1] < val:
                best[k] = (sem, val)
        out = []
        for k, (sem, val) in best.items():
            self.seen[eng][k] = val
            out.append((sem, val))
        return out

    def _commit(self, me, reads, writes):
        for b in reads:
            b.r[id(me[0])] = me
        for b in writes:
            b.w = me
            b.r = {}

    def op(self, eng, fn, reads=(), writes=()):
        waits = self._waits(eng, reads, writes)
        self.cnt[eng] += 1
        me = (self.sem[eng], self.cnt[eng], eng)
        self.prog[eng].append((waits, fn, (self.sem[eng], 1)))
        self._commit(me, reads, writes)
        return me

    def call(self, eng, meth, *args, **kw):
        reads, writes = [], []

        def unwrap(x, is_out):
            if isinstance(x, View):
                (writes if is_out else reads).append(x.buf)
                return x.ap
            return x

        a2 = [unwrap(a, i == 0) for i, a in enumerate(args)]
        k2 = {k: unwrap(v, k in ("out", "accum_out")) for k, v in kw.items()}
        return self.op(eng, lambda e: getattr(e, meth)(*a2, **k2), reads, writes)

    def dma(self, eng, out, in_, owner, group=False, fn=None, extra_reads=()):
        reads, writes = list(extra_reads), []
        o_ap = out
        i_ap = in_
        if isinstance(out, View):
            writes.append(out.buf)
            o_ap = out.ap
        if isinstance(in_, View):
            reads.append(in_.buf)
            i_ap = in_.ap
        sem = self.dma_sem(owner)
        skip = []
        if group:
            for b in writes:
                if b.w is not None and b.w[0] is sem and not b.r:
                    skip.append((b, b.w))
                    b.w = None
        waits = self._waits(eng, reads, writes)
        for b, w in skip:
            b.w = w
        owner.dcnt += 16
        me = (sem, owner.dcnt, "dma")
        if fn is None:
            fn = lambda e: e.dma_start(out=o_ap, in_=i_ap)
        self.prog[eng].append((waits, fn, (sem, 16)))
        self._commit(me, reads, writes)
        return me

    def wait_all(self, eng, events):
        best = {}
        for (sem, val, src) in events:
            k = id(sem)
            if k not in best or best[k][1] < val:
                best[k] = (sem, val)
        self.prog[eng].append((list(best.values()), None, None))

    def emit(self):
        nc = self.nc
        with nc.Block() as blk:
            for ename, deco in (("pe", blk.tensor), ("act", blk.scalar), ("dve", blk.vector),
                                ("pool", blk.gpsimd), ("sp", blk.sync)):
                prog = self.prog[ename]
                if not prog:
                    continue

                def body(e, prog=prog):
                    for waits, fn, inc in prog:
                        for (sem, val) in waits:
                            e.wait_ge(sem, val)
                        if fn is None:
                            continue
                        ins = fn(e)
                        ins.then_inc(inc[0], inc[1])

                deco(body)
        self.prog = {e: [] for e in self.ENGS}


class Rot:
    def __init__(self, items):
        self.items = items
        self.i = 0

    def next(self):
        x = self.items[self.i % len(self.items)]
        self.i += 1
        return x


def build(S):
    NT = S // 128
    TB = S // 4
    NG = S // 512
    NCH = TB // 512
    SCALE = 1.0 / math.sqrt(128.0)
    nc = bass.Bass("TRN2", target_bir_lowering=False)

    def din(name, shape, dt=F32):
        return nc.dram_tensor(name, list(shape), dt, kind="ExternalInput").ap()

    xT = din("xT", [2, D, S])
    metaT = din("metaT", [D, 16])
    wA = din("wA", [D, 512])
    lam_rows = din("lam_rows", [128, 3, 512])
    lam_cols = din("lam_cols", [128, 3, 4])
    lamB = din("lamB", [128, 3, 64])
    BT = din("BT", [128, 2, 64])
    Cpad = din("Cpad", [128, 2, 4, 128])
    dcol = din("dcol", [128, 1])
    cst = din("cst", [128, 2 + 128 + 8 + 4 * 128])
    w_ing = din("w_ing", [D, 2048])
    w_glu = din("w_glu", [1024, 2048])
    w_bs = din("w_bs", [1024, 2048])
    w_bb = din("w_bb", [1024, 2048])
    w_gate = din("w_gate", [D, 4096])
    w_out = din("w_out", [D, 2048])
    bcols = din("bcols", [128, 48])
    lnrep = din("lnrep", [128, 2, 2048])
    xTB = din("xTB", [D, TB])
    xtok = din("xtok", [TB, D])
    onehot = din("onehot", [128, 8])
    out = nc.dram_tensor("out", [TB, D], F32, kind="ExternalOutput").ap()

    uT_d = Buf(nc.dram_tensor("uT_d", [128, 16 + S], BF16).ap(), "uT_d")
    EXin_t = nc.dram_tensor("EXin", [8, 256, TB], BF16)
    EXout_t = nc.dram_tensor("EXout", [8, 8, 256, TB], BF16)
    EXin = Buf(EXin_t.ap(), "EXin")
    EXout = Buf(EXout_t.ap(), "EXout")

    WSPEC = [("w_glu", w_glu, 8, 16), ("w_ing", w_ing, NKC, 16), ("w_bs", w_bs, 8, 16), ("w_bb", w_bb, 8, 16),
             ("w_gate", w_gate, NKC, 32)]
    Wb = {}
    for (nm_, src_, nk_, ncol_) in WSPEC:
        Wb[nm_] = Buf(nc.dram_tensor(nm_ + "_b", [ncol_, 128, nk_ * 128], BF16).ap(), nm_ + "_b")
    Wb["w_out"] = Buf(nc.dram_tensor("w_out_b", [4, 128, NKC * 512], BF16).ap(), "w_out_b")

    with ExitStack() as es:
        Sx = Sched(nc, es)
        ccs = es.enter_context(nc.semaphore("ccsem"))
        sb, ps, call, dma = Sx.sb, Sx.ps, Sx.call, Sx.dma

        es_a = ExitStack()
        Sx.es_cur = es_a
        cst_f = sb([128, 2 + 128 + 8], F32, "cstf")
        cst_b = sb([128, 4, 128], BF16, "cstb")
        wA_b = sb([128, NKC, 512], BF16, "wAb")
        TW = sb([128, 2, 512], F32, "TW")
        TA = sb([128, 2, 4, 128], F32, "TA")
        Bblk = sb([128, 1024], BF16, "Bblk")
        Cb = sb([128, 2, 4, 128], BF16, "Cb")
        dcol_s = sb([128, 1], F32, "dcol")
        qT = sb([128, S], BF16, "qT")
        kT = sb([128, 16 + S], BF16, "kT")
        Vt = sb([128, NT + 1, 128], BF16, "Vt")
        zero_b = sb([128, 512], BF16, "zero")

        sidx = cst_f[:, 0:1]
        nsidx = cst_f[:, 1:2]
        taurow = cst_f[:, 2:130]
        Tri = lambda a, b_: cst_b[0:a, 0, 0:b_]
        Umat = lambda a: cst_b[0:a, 1, 0:a]
        SLmat = lambda a: cst_b[0:a, 2, 0:a]
        Mdiag = cst_b[:, 3, :]

        es0 = ExitStack()
        Sx.es_cur = es0
        dma("sp", cst_f[:, :], cst[:, 0:138], cst_f)
        dma("pool", cst_b[:, :, :], cst[:, 138:650].rearrange("p (a b) -> p a b", a=4), cst_b)
        dma("pool", wA_b[:, :, :], wA.rearrange("(kc p) c -> p kc c", p=128), wA_b)
        dma("sp", dcol_s[:, :], dcol[:, :], dcol_s)
        call("dve", "memset", zero_b[:, :], 0.0)
        lr_ = sb([128, 3, 512], F32)
        dma("sp", lr_[:, :, :], lam_rows[:, :, :], lr_)
        t_a = sb([128, 512], F32)
        t_b = sb([128, 512], F32)
        t_c = sb([128, 512], F32)
        t_d = sb([128, 512], F32)
        t_e = sb([128, 512], F32)

        ti_ = sb([128, 512], I32, "ti")
        tk_ = sb([128, 512], F32, "tk")
        tc_ = sb([128, 512], F32, "tc")
        C1 = 6.28125
        C2 = 2 * PI - 6.28125

        def _red(th, n, tmp, shift):
            ki = ti_[:, 0:n]
            kf = tk_[:, 0:n]
            cc_ = tc_[:, 0:n]
            if shift != 0.0:
                call("dve", "tensor_scalar", tmp, th, shift, None, ALU.add)
                src = tmp
            else:
                src = th
            call("dve", "tensor_scalar", kf, src, 1.0 / (2 * PI), None, ALU.mult)
            call("dve", "tensor_copy", ki, kf)
            call("dve", "tensor_copy", kf, ki)
            call("dve", "scalar_tensor_tensor", cc_, kf, -C1, src, ALU.mult, ALU.add)
            call("dve", "scalar_tensor_tensor", tmp, kf, -C2, cc_, ALU.mult, ALU.add)
            call("dve", "tensor_scalar", cc_, tmp, PI, -2 * PI, ALU.is_gt, ALU.mult)
            call("dve", "tensor_tensor", tmp, tmp, cc_, ALU.add)
            call("dve", "tensor_scalar", cc_, tmp, -PI, 2 * PI, ALU.is_lt, ALU.mult)
            call("dve", "tensor_tensor", tmp, tmp, cc_, ALU.add)
            call("dve", "tensor_scalar", tmp, tmp, 3.1415925, -3.1415925, ALU.min, ALU.max)

        def sincos(th, n, sn, cs, tmp):
            _red(th, n, tmp, 0.0)
            call("act", "activation", sn, tmp, ACT.Sin)
            _red(th, n, tmp, 0.5 * PI)
            call("act", "activation", cs, tmp, ACT.Sin)

        call("act", "activation", t_a[:, :], lr_[:, 2, :], ACT.Exp)
        call("dve", "tensor_tensor", t_b[:, :], lr_[:, 0, :], t_a[:, :], ALU.mult)
        call("dve", "tensor_tensor", t_c[:, :], lr_[:, 1, :], t_a[:, :], ALU.mult)
        call("act", "activation", t_a[:, :], t_b[:, :], ACT.Exp, scale=nsidx)
        call("dve", "tensor_scalar", t_b[:, :], t_c[:, :], sidx, None, ALU.mult)
        sincos(t_b[:, :], 512, t_c[:, :], t_d[:, :], t_e[:, :])
        call("dve", "tensor_tensor", TW[:, 0, :], t_a[:, :], t_d[:, :], ALU.mult)
        call("dve", "scalar_tensor_tensor", TW[:, 1, :], t_a[:, :], -1.0, t_c[:, :], ALU.mult, ALU.mult)
        lc = sb([128, 3, 4], F32)
        dma("sp", lc[:, :, :], lam_cols[:, :, :], lc)
        lc2 = sb([128, 3, 4], F32)
        call("act", "activation", lc2[:, 2, :], lc[:, 2, :], ACT.Exp)
        call("dve", "tensor_tensor", lc2[:, 0, :], lc[:, 0, :], lc2[:, 2, :], ALU.mult)
        call("dve", "tensor_tensor", lc2[:, 1, :], lc[:, 1, :], lc2[:, 2, :], ALU.mult)
        for pr in range(4):
            call("act", "activation", t_a[:, 0:128], taurow, ACT.Exp, scale=lc2[:, 0, pr:pr + 1])
            call("dve", "tensor_scalar", t_b[:, 0:128], taurow, lc2[:, 1, pr:pr + 1], None, ALU.mult)
            sincos(t_b[:, 0:128], 128, t_c[:, 0:128], t_d[:, 0:128], t_e[:, 0:128])
            call("dve", "tensor_tensor", TA[:, 0, pr, :], t_a[:, 0:128], t_d[:, 0:128], ALU.mult)
            call("dve", "tensor_tensor", TA[:, 1, pr, :], t_a[:, 0:128], t_c[:, 0:128], ALU.mult)
        lb = sb([128, 3, 64], F32)
        dma("sp", lb[:, :, :], lamB[:, :, :], lb)
        bt = sb([128, 2, 64], F32)
        dma("sp", bt[:, :, :], BT[:, :, :], bt)
        g = [sb([128, 64], F32, f"g{i}") for i in range(10)]
        call("act", "activation", g[0][:, :], lb[:, 2, :], ACT.Exp)
        call("dve", "tensor_tensor", g[1][:, :], lb[:, 0, :], g[0][:, :], ALU.mult)
        call("dve", "tensor_tensor", g[2][:, :], lb[:, 1, :], g[0][:, :], ALU.mult)
        call("act", "activation", g[0][:, :], g[1][:, :], ACT.Exp)
        sincos(g[2][:, :], 64, g[3][:, :], g[4][:, :], g[5][:, :])
        call("dve", "tensor_tensor", g[1][:, :], g[0][:, :], g[4][:, :], ALU.mult)
        call("dve", "tensor_tensor", g[2][:, :], g[0][:, :], g[3][:, :], ALU.mult)
        call("dve", "tensor_scalar", g[1][:, :], g[1][:, :], -1.0, None, ALU.add)
        call("dve", "tensor_tensor", g[3][:, :], lb[:, 0, :], lb[:, 0, :], ALU.mult)
        call("dve", "tensor_tensor", g[4][:, :], lb[:, 1, :], lb[:, 1, :], ALU.mult)
        call("dve", "tensor_tensor", g[3][:, :], g[3][:, :], g[4][:, :], ALU.add)
        call("dve", "reciprocal", g[3][:, :], g[3][:, :])
        call("dve", "tensor_tensor", g[4][:, :], g[1][:, :], lb[:, 0, :], ALU.mult)
        call("dve", "tensor_tensor", g[5][:, :], g[2][:, :], lb[:, 1, :], ALU.mult)
        call("dve", "tensor_tensor", g[4][:, :], g[4][:, :], g[5][:, :], ALU.add)
        call("dve", "tensor_tensor", g[4][:, :], g[4][:, :], g[3][:, :], ALU.mult)
        call("dve", "tensor_tensor", g[5][:, :], g[2][:, :], lb[:, 0, :], ALU.mult)
        call("dve", "tensor_tensor", g[6][:, :], g[1][:, :], lb[:, 1, :], ALU.mult)
        call("dve", "tensor_tensor", g[5][:, :], g[5][:, :], g[6][:, :], ALU.subtract)
        call("dve", "tensor_tensor", g[5][:, :], g[5][:, :], g[3][:, :], ALU.mult)
        call("dve", "tensor_tensor", g[6][:, :], g[4][:, :], bt[:, 0, :], ALU.mult)
        call("dve", "tensor_tensor", g[7][:, :], g[5][:, :], bt[:, 1, :], ALU.mult)
        call("dve", "tensor_tensor", g[8][:, :], g[6][:, :], g[7][:, :], ALU.subtract)
        call("dve", "tensor_tensor", g[6][:, :], g[4][:, :], bt[:, 1, :], ALU.mult)
        call("dve", "tensor_tensor", g[7][:, :], g[5][:, :], bt[:, 0, :], ALU.mult)
        call("dve", "tensor_tensor", g[9][:, :], g[6][:, :], g[7][:, :], ALU.add)
        for pr in range(4):
            for ri in range(2):
                for g2 in range(2):
                    c0 = ((pr * 2 + ri) * 2 + g2) * 64
                    mcol = cst_f[:, 130 + 2 * pr + g2:131 + 2 * pr + g2]
                    call("dve", "tensor_scalar", Bblk[:, c0:c0 + 64], g[8 + ri][:, :], mcol, None, ALU.mult)
        cp = sb([128, 2, 4, 128], F32)
        dma("sp", cp[:, :, :, :], Cpad[:, :, :, :], cp)
        call("dve", "tensor_copy", Cb[:, 0, :, :], cp[:, 0, :, :])
        call("dve", "tensor_scalar", Cb[:, 1, :, :], cp[:, 1, :, :], -1.0, None, ALU.mult)
        Sx.emit()
        es0.close()

        def convert_weights():
            for (nm_, src_, nk_, ncol_) in WSPEC:
                for m in range(ncol_):
                    dma("pool", Wb[nm_].v(Wb[nm_].t[m, :, :].rearrange("p (kc c) -> p kc c", kc=nk_)),
                        src_[:, m * 128:(m + 1) * 128].rearrange("(kc p) c -> p kc c", p=128), Wb[nm_], group=True)
            for sl in range(4):
                dma("pool", Wb["w_out"].v(Wb["w_out"].t[sl, :, :].rearrange("p (kc c) -> p kc c", kc=NKC)),
                    w_out[:, sl * 512:(sl + 1) * 512].rearrange("(kc p) c -> p kc c", p=128), Wb["w_out"], group=True)

        ex_events = []
        for b in range(2):
            esb = ExitStack()
            Sx.es_cur = esb
            xts = Rot([sb([128, NKC, 512], BF16, "xt") for _ in range(2)])
            pps = Rot([ps([128, 512], F32, "pp") for _ in range(4)])
            ust = Rot([sb([128, 512], BF16, "ust") for _ in range(2)])
            evq = Rot(["act", "dve"])
            tiles = [(-1, 16)] + [(i, 512) for i in range(NG)]
            for (ti, n) in tiles:
                xt = xts.next()
                if ti < 0:
                    src = metaT.rearrange("(kc p) t -> p kc t", p=128)
                    kcol = 0
                else:
                    src = xT[b, :, ti * 512:(ti + 1) * 512].rearrange("(kc p) t -> p kc t", p=128)
                    kcol = 16 + ti * 512
                dma("pool", xt[:, :, 0:n], src, xt)
                for m in range(3):
                    if ti < 0 and m == 1:
                        continue
                    pp = pps.next()
                    for kc in range(NKC):
                        call("pe", "matmul", pp[:, 0:n], wA_b[:, kc, m * 128:(m + 1) * 128], xt[:, kc, 0:n],
                             start=(kc == 0), stop=(kc == NKC - 1))
                    if m == 0:
                        us = ust.next()
                        call("dve", "tensor_copy", us[:, 0:n], pp[:, 0:n])
                        dma("sp", uT_d[:, kcol:kcol + n], us[:, 0:n], us)
                    elif m == 1:
                        call("act", "activation", qT[:, ti * 512:ti * 512 + n], pp[:, 0:n], ACT.Copy)
                    else:
                        call("act", "activation", kT[:, kcol:kcol + n], pp[:, 0:n], ACT.Copy)
                nsub = 1 if ti < 0 else 4
                for su in range(nsub):
                    msz = 16 if ti < 0 else 128
                    pp = pps.next()
                    for kc in range(NKC):
                        call("pe", "matmul", pp[0:msz, 0:128], xt[:, kc, su * 128:su * 128 + msz],
                             wA_b[:, kc, 384:512], start=(kc == 0), stop=(kc == NKC - 1))
                    blk = 0 if ti < 0 else 1 + ti * 4 + su
                    call("dve", "tensor_copy", Vt[0:msz, blk, :], pp[0:msz, 0:128])
            Sx.emit()
            esb.close()

            esb = ExitStack()
            Sx.es_cur = esb
            uts = Rot([sb([128, 128], BF16, "ut") for _ in range(3)])
            bus = Rot([ps([128, 1024], F32, "bu") for _ in range(2)])
            ws = Rot([sb([128, 1024], BF16, "w") for _ in range(2)])
            tm_sets = [[sb([128, 4, 128], F32, f"tm{i}_{j}") for i in range(4)] for j in range(2)]
            cps = Rot([ps([128, 2, 128], F32, "c") for _ in range(2)])
            hf = [[sb([128, 2, 128], F32, f"hf{p}_{k}") for k in range(2)] for p in range(4)]
            hb = [[sb([128, 2, 128], BF16, f"hb{p}_{k}") for k in range(2)] for p in range(4)]
            hz = sb([128, 2], F32, "hz")
            call("dve", "memset", hz[:, :], 0.0)
            s_sets = [[sb([128, 128], F32, f"st{i}_{j}") for i in range(4)] for j in range(2)]
            yps = Rot([ps([128, 128], F32, "y") for _ in range(2)])
            yv = Rot([sb([128, 128], F32, "yv") for _ in range(2)])
            ga = Rot([sb([128, 128], F32, "ga") for _ in range(2)])
            gb = Rot([sb([128, 128], F32, "gb") for _ in range(2)])
            zt = Rot([sb([128, 128], BF16, "zt") for _ in range(2)])
            carry = [(hz[:, 0:1], hz[:, 1:2]) for _ in range(4)]
            GC = math.sqrt(2.0 / PI)
            for n in range(-1, NT):
                T = 16 if n < 0 else 128
                col = 0 if n < 0 else 16 + n * 128
                par = (n + 1) % 2
                ut = uts.next()
                dma("sp", ut[:, 0:T], uT_d[:, col:col + T], ut)
                bu = bus.next()
                for hh in range(2):
                    call("pe", "matmul", bu[0:T, hh * 512:(hh + 1) * 512], ut[:, 0:T],
                         Bblk[:, hh * 512:(hh + 1) * 512], start=True, stop=True)
                w = ws.next()
                tm = tm_sets[par]
                buv = bu.v(bu.t[0:T, :].rearrange("p (a r c) -> p a r c", a=4, r=2))
                wv = w.v(w.t[0:T, :].rearrange("p (a r c) -> p a r c", a=4, r=2))
                twr = TW.v(TW.t[0:T, 0, :].rearrange("p (a c) -> p a c", a=4))
                twi = TW.v(TW.t[0:T, 1, :].rearrange("p (a c) -> p a c", a=4))
                bre = View(buv.ap[:, :, 0, :], bu)
                bim = View(buv.ap[:, :, 1, :], bu)
                call("dve", "tensor_tensor", tm[0][0:T, :, :], bre, twr, ALU.mult)
                call("dve", "tensor_tensor", tm[1][0:T, :, :], bim, twi, ALU.mult)
                call("pool", "tensor_tensor", View(wv.ap[:, :, 0, :], w), tm[0][0:T, :, :], tm[1][0:T, :, :], ALU.subtract)
                call("dve", "tensor_tensor", tm[2][0:T, :, :], bre, twi, ALU.mult)
                call("dve", "tensor_tensor", tm[3][0:T, :, :], bim, twr, ALU.mult)
                call("pool", "tensor_tensor", View(wv.ap[:, :, 1, :], w), tm[2][0:T, :, :], tm[3][0:T, :, :], ALU.add)
                for pr in range(4):
                    c = cps.next()
                    for ri in range(2):
                        call("pe", "matmul", c[:, ri, 0:T], View(wv.ap[:, pr, ri, :], w), Tri(T, T),
                             start=True, stop=True)
                    hpr, hpi = carry[pr]
                    s_t = s_sets[pr % 2]
                    h = hf[pr][par]
                    hbb = hb[pr][par]
                    call("dve", "scalar_tensor_tensor", s_t[0][:, 0:T], c[:, 0, 0:T], hpr, TA[:, 0, pr, 0:T], ALU.add, ALU.mult)
                    call("dve", "scalar_tensor_tensor", s_t[1][:, 0:T], c[:, 1, 0:T], hpi, TA[:, 1, pr, 0:T], ALU.add, ALU.mult)
                    call("pool", "tensor_tensor", h[:, 0, 0:T], s_t[0][:, 0:T], s_t[1][:, 0:T], ALU.subtract)
                    call("dve", "scalar_tensor_tensor", s_t[2][:, 0:T], c[:, 0, 0:T], hpr, TA[:, 1, pr, 0:T], ALU.add, ALU.mult)
                    call("dve", "scalar_tensor_tensor", s_t[3][:, 0:T], c[:, 1, 0:T], hpi, TA[:, 0, pr, 0:T], ALU.add, ALU.mult)
                    call("pool", "tensor_tensor", h[:, 1, 0:T], s_t[2][:, 0:T], s_t[3][:, 0:T], ALU.add)
                    carry[pr] = (h[:, 0, T - 1:T], h[:, 1, T - 1:T])
                    if n >= 0:
                        call("act", "activation", hbb[:, :, :], h[:, :, :], ACT.Copy)
                if n < 0:
                    continue
                y = yps.next()
                k = 0
                for pr in range(4):
                    for ri in range(2):
                        call("pe", "matmul", y[:, :], Cb[:, ri, pr, :], hb[pr][par][:, ri, :],
                             start=(k == 0), stop=(k == 7))
                        k += 1
                v_ = yv.next()
                a_ = ga.next()
                b_ = gb.next()
                z_ = zt.next()
                call("dve", "scalar_tensor_tensor", v_[:, :], ut[:, 0:128], dcol_s[:, 0:1], y[:, :], ALU.mult, ALU.add)
                call("act", "activation", a_[:, :], v_[:, :], ACT.Square)
                call("dve", "tensor_scalar", a_[:, :], a_[:, :], 0.044715, 1.0, ALU.mult, ALU.add)
                call("pool", "tensor_tensor", b_[:, :], a_[:, :], v_[:, :], ALU.mult)
                call("act", "activation", a_[:, :], b_[:, :], ACT.Sigmoid, scale=2.0 * GC)
                call("pool", "tensor_tensor", z_[:, :], a_[:, :], v_[:, :], ALU.mult)
                gtok = b * S + n * 128
                dd, off = gtok // TB, gtok % TB
                ex_events.append(dma("sp", EXin[dd, 0:128, off:off + 128], z_[:, :], z_))
            Sx.emit()
            esb.close()

            esb = ExitStack()
            Sx.es_cur = esb
            NW = 3
            zps = Rot([ps([128, 512], F32, "z") for _ in range(2)])
            accs = [ps([128, 512], F32, "acc") for _ in range(NW)]
            ops_ = [ps([128, 512], F32, "o") for _ in range(NW)]
            es_ = Rot([sb([128, 512], F32, "e") for _ in range(2 * NW)])
            sps = Rot([sb([128, 512], BF16, "sp") for _ in range(2 * NW)])
            ecs = Rot([sb([128, 512], F32, "ecs") for _ in range(NW)])
            ats = Rot([sb([128, 512], BF16, "at") for _ in range(NW)])
            osb = Rot([sb([128, 512], BF16, "osb") for _ in range(2)])
            if b == 0:
                convert_weights()

            def stage1(G, blk):
                kb, KS, c0, diag = blk
                q0 = G * 512
                kc0 = 0 if kb < 0 else 16 + kb * 128
                z = zps.next()
                call("pe", "matmul", z[0:KS, c0:512], kT[:, kc0:kc0 + KS], qT[:, q0 + c0:q0 + 512],
                     start=True, stop=True)
                e_ = es_.next()
                sp = sps.next()
                call("act", "activation", e_[0:KS, c0:512], z[0:KS, c0:512], ACT.Exp, scale=SCALE)
                call("act", "activation", sp[0:KS, c0:512], e_[0:KS, c0:512], ACT.Ln, bias=1.0)
                if diag:
                    call("dve", "tensor_tensor", sp[:, c0:c0 + 128], sp[:, c0:c0 + 128], Mdiag, ALU.mult)
                return (e_, sp)

            def stage2(acc, o, blk, st, last):
                kb, KS, c0, diag = blk
                e_, sp = st
                vb = 0 if kb < 0 else 1 + kb
                call("pe", "matmul", acc[0:KS, c0:512], Umat(KS), sp[0:KS, c0:512], start=False, stop=True,
                     skip_group_check=True)
                ec = ecs.next()
                at = ats.next()
                call("act", "activation", ec[0:KS, c0:512], acc[0:KS, c0:512], ACT.Exp, scale=-1.0)
                if not last:
                    call("pe", "matmul", acc[0:KS, c0:512], SLmat(KS), sp[0:KS, c0:512], start=False, stop=True,
                         skip_group_check=True)
                call("dve", "tensor_tensor", at[0:KS, c0:512], e_[0:KS, c0:512], ec[0:KS, c0:512], ALU.mult)
                if diag:
                    call("dve", "tensor_tensor", at[:, c0:c0 + 128], at[:, c0:c0 + 128], Mdiag, ALU.mult)
                call("pe", "matmul", o[:, c0:512], Vt[0:KS, vb, :], at[0:KS, c0:512], start=False, stop=True,
                     skip_group_check=True)

            for w0 in range(0, NG, NW):
                wave = list(range(w0, min(NG, w0 + NW)))
                blks = {}
                for wi, G in enumerate(wave):
                    bl = []
                    for jj in (3, 2, 1, 0):
                        bl.append((4 * G + jj, 128, 128 * jj, True))
                    for kb in range(4 * G - 1, -1, -1):
                        bl.append((kb, 128, 0, False))
                    bl.append((-1, 16, 0, False))
                    blks[G] = bl
                    call("pe", "matmul", accs[wi][:, :], zero_b[:, 0:128], zero_b[:, :], start=True, stop=True)
                    call("pe", "matmul", ops_[wi][:, :], zero_b[:, 0:128], zero_b[:, :], start=True, stop=True)
                st = {G: stage1(G, blks[G][0]) for G in wave}
                maxlen = max(len(blks[G]) for G in wave)
                for i in range(maxlen):
                    nxt = {}
                    for G in wave:
                        if i + 1 < len(blks[G]):
                            nxt[G] = stage1(G, blks[G][i + 1])
                    for wi, G in enumerate(wave):
                        if i < len(blks[G]):
                            stage2(accs[wi], ops_[wi], blks[G][i], st[G], i == len(blks[G]) - 1)
                    st = nxt
                for wi, G in enumerate(wave):
                    ob = osb.next()
                    call("act", "activation", ob[:, :], ops_[wi][:, :], ACT.Copy)
                    gtok = b * S + G * 512
                    dd, off = gtok // TB, gtok % TB
                    ex_events.append(dma("sp", EXin[dd, 128:256, off:off + 512], ob[:, :], ob))
            Sx.emit()
            esb.close()

        es_a.close()

        Sx.wait_all("pool", ex_events)

        def cc(e):
            return e.collective_compute("AllGather", ALU.bypass, replica_groups=[list(range(8))],
                                        ins=[EXin_t.ap().rearrange("d f t -> (d f) t").opt()],
                                        outs=[EXout_t.ap().rearrange("s d f t -> (s d f) t").opt()])
        Sx.prog["pool"].append(([], cc, (ccs, 1)))
        EXout.w = (ccs, 1, "cc")

        esb = ExitStack()
        Sx.es_cur = esb
        bc = sb([128, 48], F32, "bc")
        dma("sp", bc[:, :], bcols[:, :], bc)
        lnr = sb([128, 2, 2048], F32, "lnr")
        dma("sp", lnr[:, :, :], lnrep[:, :, :], lnr)
        xb = sb([128, NKC, 512], BF16, "xb")
        zT = sb([128, 8, 512], BF16, "zT")
        oT = sb([128, 8, 512], BF16, "oT")
        ysT = sb([128, 8, 512], BF16, "ysT")
        ybT = sb([128, 8, 512], BF16, "ybT")
        mxT = sb([128, NKC, 512], BF16, "mxT")
        rr = [sb([128, 2048], F32, f"rr{i}") for i in range(4)]
        w16 = Rot([sb([128, NKC, 128], BF16, "w16") for _ in range(5)])
        w8 = Rot([sb([128, 8, 128], BF16, "w8") for _ in range(6)])
        wo = Rot([sb([128, NKC, 512], BF16, "wo") for _ in range(2)])
        pB = Rot([ps([128, 512], F32, "pB") for _ in range(6)])
        sA = Rot([sb([128, 512], F32, "sA") for _ in range(2)])
        sBt = Rot([sb([128, 512], F32, "sB") for _ in range(2)])
        xtk = Rot([sb([128, 512], F32, "xtk") for _ in range(3)])
        st6 = sb([128, 4, 6], F32, "st6")
        mv = sb([128, 2], F32, "mv")
        rs = sb([128, 2], F32, "rs")
        ALPHA = 2.0 ** 0.25
        out_events = []
        oh = sb([128, 8], F32, "oh")
        dma("sp", oh[:, :], onehot[:, :], oh)
        stg = Rot([sb([128, 8, 512], BF16, "stg") for _ in range(1)])
        accf = Rot([sb([128, 512], F32, "accf") for _ in range(2)])

        wq = Rot(["sp"])

        def wload(wt, src, nk, m):
            wbuf = Wb[src]
            dma(wq.next(), wt[:, 0:nk, :], wbuf.v(wbuf.t[m, :, :].rearrange("p (kc c) -> p kc c", kc=nk)), wt)

        def lin(pp, wt, rhs, nk):
            for kc in range(nk):
                call("pe", "matmul", pp[:, :], wt[:, kc, :], rhs[:, kc, :], start=(kc == 0), stop=(kc == nk - 1))

        for ch in range(NCH):
            t0 = ch * 512
            dma("pool", xb[:, :, :], xTB[:, t0:t0 + 512].rearrange("(kc p) t -> p kc t", p=128), xb)
            for s_ in range(8):
                for (dst, f0) in ((zT, 0), (oT, 128)):
                    sg = stg.next()
                    dma("sp", sg[:, :, :], EXout.v(EXout_t.ap()[s_, :, f0:f0 + 128, t0:t0 + 512].rearrange("d p t -> p d t")), sg)
                    af = accf.next()
                    call("dve", "tensor_scalar", af[:, :], sg[:, 0, :], oh[:, 0:1], None, ALU.mult)
                    for d_ in range(1, 8):
                        dstv = dst[:, s_, :] if d_ == 7 else af[:, :]
                        call("dve", "scalar_tensor_tensor", dstv, sg[:, d_, :], oh[:, d_:d_ + 1], af[:, :], ALU.mult, ALU.add)
            for m in range(8):
                wa = w8.next(); wload(wa, "w_glu", 8, m)
                wb_ = w8.next(); wload(wb_, "w_glu", 8, 8 + m)
                wg = w16.next(); wload(wg, "w_ing", NKC, m)
                pa = pB.next(); lin(pa, wa, zT, 8)
                pb = pB.next(); lin(pb, wb_, zT, 8)
                pg = pB.next(); lin(pg, wg, xb, NKC)
                a1 = sA.next(); a2 = sBt.next()
                call("act", "activation", a1[:, :], pb[:, :], ACT.Sigmoid, bias=bc[:, 8 + m:9 + m])
                call("act", "activation", a2[:, :], pg[:, :], ACT.Silu)
                call("dve", "scalar_tensor_tensor", a1[:, :], pa[:, :], bc[:, m:m + 1], a1[:, :], ALU.add, ALU.mult)
                call("pool", "tensor_tensor", ysT[:, m, :], a1[:, :], a2[:, :], ALU.mult)
            for m in range(8):
                wg = w16.next(); wload(wg, "w_ing", NKC, 8 + m)
                pg = pB.next(); lin(pg, wg, xb, NKC)
                a2 = sBt.next()
                call("act", "activation", a2[:, :], pg[:, :], ACT.Silu)
                call("pool", "tensor_tensor", ybT[:, m, :], a2[:, :], oT[:, m, :], ALU.mult)
            for m in range(NKC):
                ws_ = w8.next(); wload(ws_, "w_bs", 8, m)
                wb_ = w8.next(); wload(wb_, "w_bb", 8, m)
                wg1 = w16.next(); wload(wg1, "w_gate", NKC, m)
                wg2 = w16.next(); wload(wg2, "w_gate", NKC, 16 + m)
                p1 = pB.next(); lin(p1, ws_, ysT, 8)
                p2 = pB.next(); lin(p2, wb_, ybT, 8)
                p3 = pB.next(); lin(p3, wg1, xb, NKC)
                p4 = pB.next(); lin(p4, wg2, xb, NKC)
                a1 = sA.next(); a2 = sBt.next()
                call("act", "activation", a1[:, :], p3[:, :], ACT.Sigmoid, bias=bc[:, 16 + m:17 + m])
                call("act", "activation", a2[:, :], p4[:, :], ACT.Sigmoid, bias=bc[:, 32 + m:33 + m])
                call("dve", "tensor_tensor", a1[:, :], a1[:, :], p1[:, :], ALU.mult)
                call("dve", "tensor_tensor", a2[:, :], a2[:, :], p2[:, :], ALU.mult)
                call("pool", "tensor_tensor", mxT[:, m, :], a1[:, :], a2[:, :], ALU.add)
            for sl in range(4):
                wt = wo.next()
                dma(wq.next(), wt[:, :, :], Wb["w_out"].v(Wb["w_out"].t[sl, :, :].rearrange("p (kc c) -> p kc c", kc=NKC)), wt)
                for t in range(4):
                    pp = pB.next()
                    for kc in range(NKC):
                        call("pe", "matmul", pp[:, :], mxT[:, kc, t * 128:(t + 1) * 128], wt[:, kc, :],
                             start=(kc == 0), stop=(kc == NKC - 1))
                    xk = xtk.next()
                    dma("sp", xk[:, :], xtok[t0 + t * 128:t0 + (t + 1) * 128, sl * 512:(sl + 1) * 512], xk)
                    call("dve", "scalar_tensor_tensor", rr[t][:, sl * 512:(sl + 1) * 512], xk[:, :], ALPHA, pp[:, :],
                         ALU.mult, ALU.add)
            for t in range(4):
                r = rr[t]
                for sl in range(4):
                    call("dve", "bn_stats", st6[:, sl, :], r[:, sl * 512:(sl + 1) * 512])
                call("dve", "bn_aggr", mv[:, :], st6[:, :, :].ap if False else st6.v(st6.t[:, :, :].rearrange("p a b -> p (a b)")))
                call("dve", "tensor_scalar", rs[:, 0:1], mv[:, 1:2], 1e-5, None, ALU.add)
                call("act", "activation", rs[:, 0:1], rs[:, 0:1], ACT.Sqrt)
                call("dve", "reciprocal", rs[:, 0:1], rs[:, 0:1])
                call("dve", "scalar_tensor_tensor", rs[:, 1:2], mv[:, 0:1], -1.0, rs[:, 0:1], ALU.mult, ALU.mult)
                call("act", "activation", r[:, :], r[:, :], ACT.Identity, bias=rs[:, 1:2], scale=rs[:, 0:1])
                call("dve", "tensor_tensor", r[:, :], r[:, :], lnr[:, 0, :], ALU.mult)
                call("pool", "tensor_tensor", r[:, :], r[:, :], lnr[:, 1, :], ALU.add)
                out_events.append(dma("sp", out[t0 + t * 128:t0 + (t + 1) * 128, :], r[:, :], r))
        Sx.wait_all("sp", out_events)
        Sx.emit()
        esb.close()
    return nc


def _prep(inputs, S):
    f = np.float32
    x = np.asarray(inputs["x"], f)
    TB = S // 4
    xTall = np.ascontiguousarray(np.transpose(x, (0, 2, 1)))
    metaT = np.ascontiguousarray(np.asarray(inputs["meta_tokens"], f).T)
    w_in = np.asarray(inputs["w_in"], f)[0]
    lre = np.asarray(inputs["ssm_lambda_re"], f)[0]
    lim = np.asarray(inputs["ssm_lambda_im"], f)[0]
    ldt = np.asarray(inputs["ssm_log_dt"], f)[0]
    bre = np.asarray(inputs["ssm_b_re"], f)[0]
    bim = np.asarray(inputs["ssm_b_im"], f)[0]
    cre = np.asarray(inputs["ssm_c_re"], f)[0]
    cim = np.asarray(inputs["ssm_c_im"], f)[0]
    dsk = np.asarray(inputs["ssm_d"], f)[0]
    cst = np.zeros((128, 650), f)
    cst[:, 0] = np.arange(1, 129)
    cst[:, 1] = -np.arange(1, 129)
    cst[:, 2:130] = np.arange(1, 129)[None, :]
    for g8 in range(8):
        cst[16 * g8:16 * g8 + 16, 130 + g8] = 1.0
    ii = np.arange(128)
    cst[:, 138:266] = (ii[:, None] <= ii[None, :])
    cst[:, 266:394] = (ii[:, None] >= ii[None, :])
    cst[:, 394:522] = (ii[:, None] < ii[None, :])
    cst[:, 522:650] = (ii[:, None] < ii[None, :])
    w_ing = np.ascontiguousarray(np.concatenate([w_in[:, 1024:2048], w_in[:, 5120:6144]], axis=1))
    bg = np.asarray(inputs["b_glu"], f)[0]
    bgate = np.asarray(inputs["b_gate"], f)[0]
    bcols = np.ascontiguousarray(np.concatenate([bg.reshape(16, 128).T, bgate.reshape(32, 128).T], axis=1))
    lnrep = np.ascontiguousarray(np.stack([
        np.broadcast_to(np.asarray(inputs["ln_gain"], f)[0][None, :], (128, 2048)),
        np.broadcast_to(np.asarray(inputs["ln_bias"], f)[0][None, :], (128, 2048))], axis=1))
    shared = dict(xT=xTall, metaT=metaT, cst=cst, w_ing=w_ing,
                  w_glu=np.asarray(inputs["w_glu"], f)[0], w_bs=np.asarray(inputs["w_branch_ssm"], f)[0],
                  w_bb=np.asarray(inputs["w_branch_sb"], f)[0], w_gate=np.asarray(inputs["w_gate"], f)[0],
                  w_out=np.asarray(inputs["w_out"], f)[0], bcols=bcols, lnrep=lnrep)
    maps = []
    for c in range(8):
        gs = slice(8 * c, 8 * c + 8)
        cols = np.concatenate([np.arange(128 * c, 128 * c + 128), 2048 + np.arange(128 * c, 128 * c + 128),
                               3072 + np.arange(128 * c, 128 * c + 128), 4096 + np.arange(128 * c, 128 * c + 128)])
        wA = np.ascontiguousarray(w_in[:, cols])
        lr8, li8, ld8 = lre[gs], lim[gs], ldt[gs]
        ld8p = np.repeat(ld8[:, None], 64, axis=1)
        rows = np.stack([lr8.reshape(512), li8.reshape(512), ld8p.reshape(512)], axis=0)
        lam_rows = np.ascontiguousarray(np.broadcast_to(rows[None], (128, 3, 512)))
        def colify(a):
            return a.reshape(4, 2, 64).transpose(1, 2, 0).reshape(128, 4)
        lam_cols = np.ascontiguousarray(np.stack([colify(lr8), colify(li8), colify(ld8p)], axis=1))
        lamB = np.ascontiguousarray(np.stack([np.repeat(lr8, 16, axis=0), np.repeat(li8, 16, axis=0),
                                              np.repeat(ld8p, 16, axis=0)], axis=1))
        BTc = np.ascontiguousarray(np.stack([bre[gs].transpose(0, 2, 1).reshape(128, 64),
                                             bim[gs].transpose(0, 2, 1).reshape(128, 64)], axis=1))
        Cp = np.zeros((128, 2, 4, 128), f)
        for pr in range(4):
            for g2 in range(2):
                g8 = 2 * pr + g2
                Cp[64 * g2:64 * g2 + 64, 0, pr, 16 * g8:16 * g8 + 16] = cre[8 * c + g8].T
                Cp[64 * g2:64 * g2 + 64, 1, pr, 16 * g8:16 * g8 + 16] = cim[8 * c + g8].T
        dcol = np.ascontiguousarray(dsk[gs].reshape(128, 1))
        b_, j_ = c // 4, c % 4
        xTB = np.ascontiguousarray(xTall[b_][:, j_ * TB:(j_ + 1) * TB])
        xtok = np.ascontiguousarray(x[b_, j_ * TB:(j_ + 1) * TB, :])
        m = dict(shared)
        m.update(wA=wA, lam_rows=lam_rows, lam_cols=lam_cols, lamB=lamB, BT=BTc, Cpad=Cp, dcol=dcol,
                 xTB=xTB, xtok=xtok, onehot=np.ascontiguousarray(np.broadcast_to(np.eye(8, dtype=f)[c][None, :], (128, 8))))
        maps.append(m)
    return maps


def kernel(**inputs):
    x = np.asarray(inputs["x"])
    S = x.shape[1]
    TB = S // 4
    nc = build(S)
    maps = _prep(inputs, S)
    res = run_bass_kernel_spmd(nc, maps, core_ids=list(range(8)))
    out = np.zeros((2, S, D), np.float32)
    for c in range(8):
        out[c // 4, (c % 4) * TB:(c % 4 + 1) * TB, :] = res.results[c]["out"]
    return out
```

```python
import math
from contextlib import ExitStack

import numpy as np
import concourse.bass as bass
import concourse.mybir as mybir
from concourse.bass_utils import run_bass_kernel_spmd

ACT = mybir.ActivationFunctionType
ALU = mybir.AluOpType
F32 = mybir.dt.float32
BF16 = mybir.dt.bfloat16
I32 = mybir.dt.int32

D = 2048
NKC = 16
PI = math.pi


class View:
    def __init__(self, ap, buf):
        self.ap = ap
        self.buf = buf


class Buf:
    def __init__(self, t, name):
        self.t = t
        self.name = name
        self.w = None
        self.r = {}
        self.dsem = None
        self.dcnt = 0

    def __getitem__(self, idx):
        return View(self.t[idx], self)

    def v(self, ap):
        return View(ap, self)


class Sched:
    ENGS = ("pe", "act", "dve", "pool", "sp")

    def __init__(self, nc, es):
        self.nc = nc
        self.es = es
        self.es_cur = es
        self.sem = {e: es.enter_context(nc.semaphore("s_" + e)) for e in self.ENGS}
        self.cnt = {e: 0 for e in self.ENGS}
        self.seen = {e: {} for e in self.ENGS}
        self.prog = {e: [] for e in self.ENGS}
        self.nbuf = 0

    def sb(self, shape, dt, name=None):
        self.nbuf += 1
        name = (name or "sb") + f"_{self.nbuf}"
        t = self.es_cur.enter_context(self.nc.sbuf_tensor(name, list(shape), dt))
        return Buf(t, name)

    def ps(self, shape, dt=F32, name=None):
        self.nbuf += 1
        name = (name or "ps") + f"_{self.nbuf}"
        t = self.es_cur.enter_context(self.nc.psum_tensor(name, list(shape), dt))
        return Buf(t, name)

    def dma_sem(self, buf):
        if buf.dsem is None:
            buf.dsem = self.es.enter_context(self.nc.semaphore("d_" + buf.name))
        return buf.dsem

    def _waits(self, eng, reads, writes):
        ev = []
        for b in reads:
            if b.w is not None:
                ev.append(b.w)
        for b in writes:
            if b.w is not None:
                ev.append(b.w)
            ev.extend(b.r.values())
        best = {}
        for (sem, val, src) in ev:
            if src == "pe" and eng == "pe":
                continue
            k = id(sem)
            if self.seen[eng].get(k, 0) >= val:
                continue
            if k not in best or best[k][1] < val:
                best[k] = (sem, val)
        out = []
        for k, (sem, val) in best.items():
            self.seen[eng][k] = val
            out.append((sem, val))
        return out

    def _commit(self, me, reads, writes):
        for b in reads:
            b.r[id(me[0])] = me
        for b in writes:
            b.w = me
            b.r = {}

    def op(self, eng, fn, reads=(), writes=()):
        waits = self._waits(eng, reads, writes)
        self.cnt[eng] += 1
        me = (self.sem[eng], self.cnt[eng], eng)
        self.prog[eng].append((waits, fn, (self.sem[eng], 1)))
        self._commit(me, reads, writes)
        return me

    def call(self, eng, meth, *args, **kw):
        reads, writes = [], []

        def unwrap(x, is_out):
            if isinstance(x, View):
                (writes if is_out else reads).append(x.buf)
                return x.ap
            return x

        a2 = [unwrap(a, i == 0) for i, a in enumerate(args)]
        k2 = {k: unwrap(v, k in ("out", "accum_out")) for k, v in kw.items()}
        return self.op(eng, lambda e: getattr(e, meth)(*a2, **k2), reads, writes)

    def dma(self, eng, out, in_, owner, group=False, fn=None, extra_reads=()):
        reads, writes = list(extra_reads), []
        o_ap = out
        i_ap = in_
        if isinstance(out, View):
            writes.append(out.buf)
            o_ap = out.ap
        if isinstance(in_, View):
            reads.append(in_.buf)
            i_ap = in_.ap
        sem = self.dma_sem(owner)
        skip = []
        if group:
            for b in writes:
                if b.w is not None and b.w[0] is sem and not b.r:
                    skip.append((b, b.w))
                    b.w = None
        waits = self._waits(eng, reads, writes)
        for b, w in skip:
            b.w = w
        owner.dcnt += 16
        me = (sem, owner.dcnt, "dma")
        if fn is None:
            fn = lambda e: e.dma_start(out=o_ap, in_=i_ap)
        self.prog[eng].append((waits, fn, (sem, 16)))
        self._commit(me, reads, writes)
        return me

    def wait_all(self, eng, events):
        best = {}
        for (sem, val, src) in events:
            k = id(sem)
            if k not in best or best[k][1] < val:
                best[k] = (sem, val)
        self.prog[eng].append((list(best.values()), None, None))

    def emit(self):
        nc = self.nc
        with nc.Block() as blk:
            for ename, deco in (("pe", blk.tensor), ("act", blk.scalar), ("dve", blk.vector),
                                ("pool", blk.gpsimd), ("sp", blk.sync)):
                prog = self.prog[ename]
                if not prog:
                    continue

                def body(e, prog=prog):
                    for waits, fn, inc in prog:
                        for (sem, val) in waits:
                            e.wait_ge(sem, val)
                        if fn is None:
                            continue
                        ins = fn(e)
                        ins.then_inc(inc[0], inc[1])

                deco(body)
        self.prog = {e: [] for e in self.ENGS}


class Rot:
    def __init__(self, items):
        self.items = items
        self.i = 0

    def next(self):
        x = self.items[self.i % len(self.items)]
        self.i += 1
        return x


def build(S):
    NT = S // 128
    TB = S // 4
    NG = S // 512
    NCH = TB // 512
    SCALE = 1.0 / math.sqrt(128.0)
    nc = bass.Bass("TRN2", target_bir_lowering=False)

    def din(name, shape, dt=F32):
        return nc.dram_tensor(name, list(shape), dt, kind="ExternalInput").ap()

    xT = din("xT", [2, D, S])
    metaT = din("metaT", [D, 16])
    wA = din("wA", [D, 512])
    lam_rows = din("lam_rows", [128, 3, 512])
    lam_cols = din("lam_cols", [128, 3, 4])
    lamB = din("lamB", [128, 3, 64])
    BT = din("BT", [128, 2, 64])
    Cpad = din("Cpad", [128, 2, 4, 128])
    dcol = din("dcol", [128, 1])
    cst = din("cst", [128, 2 + 128 + 8 + 4 * 128])
    w_ing = din("w_ing", [D, 2048])
    w_glu = din("w_glu", [1024, 2048])
    w_bs = din("w_bs", [1024, 2048])
    w_bb = din("w_bb", [1024, 2048])
    w_gate = din("w_gate", [D, 4096])
    w_out = din("w_out", [D, 2048])
    bcols = din("bcols", [128, 48])
    lnrep = din("lnrep", [128, 2, 2048])
    xTB = din("xTB", [D, TB])
    xtok = din("xtok", [TB, D])
    onehot = din("onehot", [128, 8])
    out = nc.dram_tensor("out", [TB, D], F32, kind="ExternalOutput").ap()

    uT_d = Buf(nc.dram_tensor("uT_d", [128, 16 + S], BF16).ap(), "uT_d")
    EXin_t = nc.dram_tensor("EXin", [8, 256, TB], BF16)
    EXout_t = nc.dram_tensor("EXout", [8, 8, 256, TB], BF16)
    EXin = Buf(EXin_t.ap(), "EXin")
    EXout = Buf(EXout_t.ap(), "EXout")

    WSPEC = [("w_glu", w_glu, 8, 16), ("w_ing", w_ing, NKC, 16), ("w_bs", w_bs, 8, 16), ("w_bb", w_bb, 8, 16),
             ("w_gate", w_gate, NKC, 32)]
    Wb = {}
    for (nm_, src_, nk_, ncol_) in WSPEC:
        Wb[nm_] = Buf(nc.dram_tensor(nm_ + "_b", [ncol_, 128, nk_ * 128], BF16).ap(), nm_ + "_b")
    Wb["w_out"] = Buf(nc.dram_tensor("w_out_b", [4, 128, NKC * 512], BF16).ap(), "w_out_b")

    with ExitStack() as es:
        Sx = Sched(nc, es)
        ccs = es.enter_context(nc.semaphore("ccsem"))
        sb, ps, call, dma = Sx.sb, Sx.ps, Sx.call, Sx.dma

        es_a = ExitStack()
        Sx.es_cur = es_a
        cst_f = sb([128, 2 + 128 + 8], F32, "cstf")
        cst_b = sb([128, 4, 128], BF16, "cstb")
        wA_b = sb([128, NKC, 512], BF16, "wAb")
        TW = sb([128, 2, 512], F32, "TW")
        TA = sb([128, 2, 4, 128], F32, "TA")
        Bblk = sb([128, 1024], BF16, "Bblk")
        Cb = sb([128, 2, 4, 128], BF16, "Cb")
        dcol_s = sb([128, 1], F32, "dcol")
        qT = sb([128, S], BF16, "qT")
        kT = sb([128, 16 + S], BF16, "kT")
        Vt = sb([128, NT + 1, 128], BF16, "Vt")
        zero_b = sb([128, 512], BF16, "zero")

        sidx = cst_f[:, 0:1]
        nsidx = cst_f[:, 1:2]
        taurow = cst_f[:, 2:130]
        Tri = lambda a, b_: cst_b[0:a, 0, 0:b_]
        Umat = lambda a: cst_b[0:a, 1, 0:a]
        SLmat = lambda a: cst_b[0:a, 2, 0:a]
        Mdiag = cst_b[:, 3, :]

        es0 = ExitStack()
        Sx.es_cur = es0
        dma("sp", cst_f[:, :], cst[:, 0:138], cst_f)
        dma("pool", cst_b[:, :, :], cst[:, 138:650].rearrange("p (a b) -> p a b", a=4), cst_b)
        dma("pool", wA_b[:, :, :], wA.rearrange("(kc p) c -> p kc c", p=128), wA_b)
        dma("sp", dcol_s[:, :], dcol[:, :], dcol_s)
        call("dve", "memset", zero_b[:, :], 0.0)
        lr_ = sb([128, 3, 512], F32)
        dma("sp", lr_[:, :, :], lam_rows[:, :, :], lr_)
        t_a = sb([128, 512], F32)
        t_b = sb([128, 512], F32)
        t_c = sb([128, 512], F32)
        t_d = sb([128, 512], F32)
        t_e = sb([128, 512], F32)

        ti_ = sb([128, 512], I32, "ti")
        tk_ = sb([128, 512], F32, "tk")
        tc_ = sb([128, 512], F32, "tc")
        C1 = 6.28125
        C2 = 2 * PI - 6.28125

        def _red(th, n, tmp, shift):
            ki = ti_[:, 0:n]
            kf = tk_[:, 0:n]
            cc_ = tc_[:, 0:n]
            if shift != 0.0:
                call("dve", "tensor_scalar", tmp, th, shift, None, ALU.add)
                src = tmp
            else:
                src = th
            call("dve", "tensor_scalar", kf, src, 1.0 / (2 * PI), None, ALU.mult)
            call("dve", "tensor_copy", ki, kf)
            call("dve", "tensor_copy", kf, ki)
            call("dve", "scalar_tensor_tensor", cc_, kf, -C1, src, ALU.mult, ALU.add)
            call("dve", "scalar_tensor_tensor", tmp, kf, -C2, cc_, ALU.mult, ALU.add)
            call("dve", "tensor_scalar", cc_, tmp, PI, -2 * PI, ALU.is_gt, ALU.mult)
            call("dve", "tensor_tensor", tmp, tmp, cc_, ALU.add)
            call("dve", "tensor_scalar", cc_, tmp, -PI, 2 * PI, ALU.is_lt, ALU.mult)
            call("dve", "tensor_tensor", tmp, tmp, cc_, ALU.add)
            call("dve", "tensor_scalar", tmp, tmp, 3.1415925, -3.1415925, ALU.min, ALU.max)

        def sincos(th, n, sn, cs, tmp):
            _red(th, n, tmp, 0.0)
            call("act", "activation", sn, tmp, ACT.Sin)
            _red(th, n, tmp, 0.5 * PI)
            call("act", "activation", cs, tmp, ACT.Sin)

        call("act", "activation", t_a[:, :], lr_[:, 2, :], ACT.Exp)
        call("dve", "tensor_tensor", t_b[:, :], lr_[:, 0, :], t_a[:, :], ALU.mult)
        call("dve", "tensor_tensor", t_c[:, :], lr_[:, 1, :], t_a[:, :], ALU.mult)
        call("act", "activation", t_a[:, :], t_b[:, :], ACT.Exp, scale=nsidx)
        call("dve", "tensor_scalar", t_b[:, :], t_c[:, :], sidx, None, ALU.mult)
        sincos(t_b[:, :], 512, t_c[:, :], t_d[:, :], t_e[:, :])
        call("dve", "tensor_tensor", TW[:, 0, :], t_a[:, :], t_d[:, :], ALU.mult)
        call("dve", "scalar_tensor_tensor", TW[:, 1, :], t_a[:, :], -1.0, t_c[:, :], ALU.mult, ALU.mult)
        lc = sb([128, 3, 4], F32)
        dma("sp", lc[:, :, :], lam_cols[:, :, :], lc)
        lc2 = sb([128, 3, 4], F32)
        call("act", "activation", lc2[:, 2, :], lc[:, 2, :], ACT.Exp)
        call("dve", "tensor_tensor", lc2[:, 0, :], lc[:, 0, :], lc2[:, 2, :], ALU.mult)
        call("dve", "tensor_tensor", lc2[:, 1, :], lc[:, 1, :], lc2[:, 2, :], ALU.mult)
        for pr in range(4):
            call("act", "activation", t_a[:, 0:128], taurow, ACT.Exp, scale=lc2[:, 0, pr:pr + 1])
            call("dve", "tensor_scalar", t_b[:, 0:128], taurow, lc2[:, 1, pr:pr + 1], None, ALU.mult)
            sincos(t_b[:, 0:128], 128, t_c[:, 0:128], t_d[:, 0:128], t_e[:, 0:128])
            call("dve", "tensor_tensor", TA[:, 0, pr, :], t_a[:, 0:128], t_d[:, 0:128], ALU.mult)
            call("dve", "tensor_tensor", TA[:, 1, pr, :], t_a[:, 0:128], t_c[:, 0:128], ALU.mult)
        lb = sb([128, 3, 64], F32)
        dma("sp", lb[:, :, :], lamB[:, :, :], lb)
        bt = sb([128, 2, 64], F32)
        dma("sp", bt[:, :, :], BT[:, :, :], bt)
        g = [sb([128, 64], F32, f"g{i}") for i in range(10)]
        call("act", "activation", g[0][:, :], lb[:, 2, :], ACT.Exp)
        call("dve", "tensor_tensor", g[1][:, :], lb[:, 0, :], g[0][:, :], ALU.mult)
        call("dve", "tensor_tensor", g[2][:, :], lb[:, 1, :], g[0][:, :], ALU.mult)
        call("act", "activation", g[0][:, :], g[1][:, :], ACT.Exp)
        sincos(g[2][:, :], 64, g[3][:, :], g[4][:, :], g[5][:, :])
        call("dve", "tensor_tensor", g[1][:, :], g[0][:, :], g[4][:, :], ALU.mult)
        call("dve", "tensor_tensor", g[2][:, :], g[0][:, :], g[3][:, :], ALU.mult)
        call("dve", "tensor_scalar", g[1][:, :], g[1][:, :], -1.0, None, ALU.add)
        call("dve", "tensor_tensor", g[3][:, :], lb[:, 0, :], lb[:, 0, :], ALU.mult)
        call("dve", "tensor_tensor", g[4][:, :], lb[:, 1, :], lb[:, 1, :], ALU.mult)
        call("dve", "tensor_tensor", g[3][:, :], g[3][:, :], g[4][:, :], ALU.add)
        call("dve", "reciprocal", g[3][:, :], g[3][:, :])
        call("dve", "tensor_tensor", g[4][:, :], g[1][:, :], lb[:, 0, :], ALU.mult)
        call("dve", "tensor_tensor", g[5][:, :], g[2][:, :], lb[:, 1, :], ALU.mult)
        call("dve", "tensor_tensor", g[4][:, :], g[4][:, :], g[5][:, :], ALU.add)
        call("dve", "tensor_tensor", g[4][:, :], g[4][:, :], g[3][:, :], ALU.mult)
        call("dve", "tensor_tensor", g[5][:, :], g[2][:, :], lb[:, 0, :], ALU.mult)
        call("dve", "tensor_tensor", g[6][:, :], g[1][:, :], lb[:, 1, :], ALU.mult)
        call("dve", "tensor_tensor", g[5][:, :], g[5][:, :], g[6][:, :], ALU.subtract)
        call("dve", "tensor_tensor", g[5][:, :], g[5][:, :], g[3][:, :], ALU.mult)
        call("dve", "tensor_tensor", g[6][:, :], g[4][:, :], bt[:, 0, :], ALU.mult)
        call("dve", "tensor_tensor", g[7][:, :], g[5][:, :], bt[:, 1, :], ALU.mult)
        call("dve", "tensor_tensor", g[8][:, :], g[6][:, :], g[7][:, :], ALU.subtract)
        call("dve", "tensor_tensor", g[6][:, :], g[4][:, :], bt[:, 1, :], ALU.mult)
        call("dve", "tensor_tensor", g[7][:, :], g[5][:, :], bt[:, 0, :], ALU.mult)
        call("dve", "tensor_tensor", g[9][:, :], g[6][:, :], g[7][:, :], ALU.add)
        for pr in range(4):
            for ri in range(2):
                for g2 in range(2):
                    c0 = ((pr * 2 + ri) * 2 + g2) * 64
                    mcol = cst_f[:, 130 + 2 * pr + g2:131 + 2 * pr + g2]
                    call("dve", "tensor_scalar", Bblk[:, c0:c0 + 64], g[8 + ri][:, :], mcol, None, ALU.mult)
        cp = sb([128, 2, 4, 128], F32)
        dma("sp", cp[:, :, :, :], Cpad[:, :, :, :], cp)
        call("dve", "tensor_copy", Cb[:, 0, :, :], cp[:, 0, :, :])
        call("dve", "tensor_scalar", Cb[:, 1, :, :], cp[:, 1, :, :], -1.0, None, ALU.mult)
        Sx.emit()
        es0.close()

        def convert_weights():
            for (nm_, src_, nk_, ncol_) in WSPEC:
                for m in range(ncol_):
                    dma("pool", Wb[nm_].v(Wb[nm_].t[m, :, :].rearrange("p (kc c) -> p kc c", kc=nk_)),
                        src_[:, m * 128:(m + 1) * 128].rearrange("(kc p) c -> p kc c", p=128), Wb[nm_], group=True)
            for sl in range(4):
                dma("pool", Wb["w_out"].v(Wb["w_out"].t[sl, :, :].rearrange("p (kc c) -> p kc c", kc=NKC)),
                    w_out[:, sl * 512:(sl + 1) * 512].rearrange("(kc p) c -> p kc c", p=128), Wb["w_out"], group=True)

        ex_events = []
        for b in range(2):
            esb = ExitStack()
            Sx.es_cur = esb
            xts = Rot([sb([128, NKC, 512], BF16, "xt") for _ in range(2)])
            pps = Rot([ps([128, 512], F32, "pp") for _ in range(4)])
            ust = Rot([sb([128, 512], BF16, "ust") for _ in range(2)])
            evq = Rot(["act", "dve"])
            tiles = [(-1, 16)] + [(i, 512) for i in range(NG)]
            for (ti, n) in tiles:
                xt = xts.next()
                if ti < 0:
                    src = metaT.rearrange("(kc p) t -> p kc t", p=128)
                    kcol = 0
                else:
                    src = xT[b, :, ti * 512:(ti + 1) * 512].rearrange("(kc p) t -> p kc t", p=128)
                    kcol = 16 + ti * 512
                dma("pool", xt[:, :, 0:n], src, xt)
                for m in range(3):
                    if ti < 0 and m == 1:
                        continue
                    pp = pps.next()
                    for kc in range(NKC):
                        call("pe", "matmul", pp[:, 0:n], wA_b[:, kc, m * 128:(m + 1) * 128], xt[:, kc, 0:n],
                             start=(kc == 0), stop=(kc == NKC - 1))
                    if m == 0:
                        us = ust.next()
                        call("dve", "tensor_copy", us[:, 0:n], pp[:, 0:n])
                        dma("sp", uT_d[:, kcol:kcol + n], us[:, 0:n], us)
                    elif m == 1:
                        call("act", "activation", qT[:, ti * 512:ti * 512 + n], pp[:, 0:n], ACT.Copy)
                    else:
                        call("act", "activation", kT[:, kcol:kcol + n], pp[:, 0:n], ACT.Copy)
                nsub = 1 if ti < 0 else 4
                for su in range(nsub):
                    msz = 16 if ti < 0 else 128
                    pp = pps.next()
                    for kc in range(NKC):
                        call("pe", "matmul", pp[0:msz, 0:128], xt[:, kc, su * 128:su * 128 + msz],
                             wA_b[:, kc, 384:512], start=(kc == 0), stop=(kc == NKC - 1))
                    blk = 0 if ti < 0 else 1 + ti * 4 + su
                    call("dve", "tensor_copy", Vt[0:msz, blk, :], pp[0:msz, 0:128])
            Sx.emit()
            esb.close()

            esb = ExitStack()
            Sx.es_cur = esb
            uts = Rot([sb([128, 128], BF16, "ut") for _ in range(3)])
            bus = Rot([ps([128, 1024], F32, "bu") for _ in range(2)])
            ws = Rot([sb([128, 1024], BF16, "w") for _ in range(2)])
            tm_sets = [[sb([128, 4, 128], F32, f"tm{i}_{j}") for i in range(4)] for j in range(2)]
            cps = Rot([ps([128, 2, 128], F32, "c") for _ in range(2)])
            hf = [[sb([128, 2, 128], F32, f"hf{p}_{k}") for k in range(2)] for p in range(4)]
            hb = [[sb([128, 2, 128], BF16, f"hb{p}_{k}") for k in range(2)] for p in range(4)]
            hz = sb([128, 2], F32, "hz")
            call("dve", "memset", hz[:, :], 0.0)
            s_sets = [[sb([128, 128], F32, f"st{i}_{j}") for i in range(4)] for j in range(2)]
            yps = Rot([ps([128, 128], F32, "y") for _ in range(2)])
            yv = Rot([sb([128, 128], F32, "yv") for _ in range(2)])
            ga = Rot([sb([128, 128], F32, "ga") for _ in range(2)])
            gb = Rot([sb([128, 128], F32, "gb") for _ in range(2)])
            zt = Rot([sb([128, 128], BF16, "zt") for _ in range(2)])
            carry = [(hz[:, 0:1], hz[:, 1:2]) for _ in range(4)]
            GC = math.sqrt(2.0 / PI)
            for n in range(-1, NT):
                T = 16 if n < 0 else 128
                col = 0 if n < 0 else 16 + n * 128
                par = (n + 1) % 2
                ut = uts.next()
                dma("sp", ut[:, 0:T], uT_d[:, col:col + T], ut)
                bu = bus.next()
                for hh in range(2):
                    call("pe", "matmul", bu[0:T, hh * 512:(hh + 1) * 512], ut[:, 0:T],
                         Bblk[:, hh * 512:(hh + 1) * 512], start=True, stop=True)
                w = ws.next()
                tm = tm_sets[par]
                buv = bu.v(bu.t[0:T, :].rearrange("p (a r c) -> p a r c", a=4, r=2))
                wv = w.v(w.t[0:T, :].rearrange("p (a r c) -> p a r c", a=4, r=2))
                twr = TW.v(TW.t[0:T, 0, :].rearrange("p (a c) -> p a c", a=4))
                twi = TW.v(TW.t[0:T, 1, :].rearrange("p (a c) -> p a c", a=4))
                bre = View(buv.ap[:, :, 0, :], bu)
                bim = View(buv.ap[:, :, 1, :], bu)
                call("dve", "tensor_tensor", tm[0][0:T, :, :], bre, twr, ALU.mult)
                call("dve", "tensor_tensor", tm[1][0:T, :, :], bim, twi, ALU.mult)
                call("pool", "tensor_tensor", View(wv.ap[:, :, 0, :], w), tm[0][0:T, :, :], tm[1][0:T, :, :], ALU.subtract)
                call("dve", "tensor_tensor", tm[2][0:T, :, :], bre, twi, ALU.mult)
                call("dve", "tensor_tensor", tm[3][0:T, :, :], bim, twr, ALU.mult)
                call("pool", "tensor_tensor", View(wv.ap[:, :, 1, :], w), tm[2][0:T, :, :], tm[3][0:T, :, :], ALU.add)
                for pr in range(4):
                    c = cps.next()
                    for ri in range(2):
                        call("pe", "matmul", c[:, ri, 0:T], View(wv.ap[:, pr, ri, :], w), Tri(T, T),
                             start=True, stop=True)
                    hpr, hpi = carry[pr]
                    s_t = s_sets[pr % 2]
                    h = hf[pr][par]
                    hbb = hb[pr][par]
                    call("dve", "scalar_tensor_tensor", s_t[0][:, 0:T], c[:, 0, 0:T], hpr, TA[:, 0, pr, 0:T], ALU.add, ALU.mult)
                    call("dve", "scalar_tensor_tensor", s_t[1][:, 0:T], c[:, 1, 0:T], hpi, TA[:, 1, pr, 0:T], ALU.add, ALU.mult)
                    call("pool", "tensor_tensor", h[:, 0, 0:T], s_t[0][:, 0:T], s_t[1][:, 0:T], ALU.subtract)
                    call("dve", "scalar_tensor_tensor", s_t[2][:, 0:T], c[:, 0, 0:T], hpr, TA[:, 1, pr, 0:T], ALU.add, ALU.mult)
                    call("dve", "scalar_tensor_tensor", s_t[3][:, 0:T], c[:, 1, 0:T], hpi, TA[:, 0, pr, 0:T], ALU.add, ALU.mult)
                    call("pool", "tensor_tensor", h[:, 1, 0:T], s_t[2][:, 0:T], s_t[3][:, 0:T], ALU.add)
                    carry[pr] = (h[:, 0, T - 1:T], h[:, 1, T - 1:T])
                    if n >= 0:
                        call("act", "activation", hbb[:, :, :], h[:, :, :], ACT.Copy)
                if n < 0:
                    continue
                y = yps.next()
                k = 0
                for pr in range(4):
                    for ri in range(2):
                        call("pe", "matmul", y[:, :], Cb[:, ri, pr, :], hb[pr][par][:, ri, :],
                             start=(k == 0), stop=(k == 7))
                        k += 1
                v_ = yv.next()
                a_ = ga.next()
                b_ = gb.next()
                z_ = zt.next()
                call("dve", "scalar_tensor_tensor", v_[:, :], ut[:, 0:128], dcol_s[:, 0:1], y[:, :], ALU.mult, ALU.add)
                call("act", "activation", a_[:, :], v_[:, :], ACT.Square)
                call("dve", "tensor_scalar", a_[:, :], a_[:, :], 0.044715, 1.0, ALU.mult, ALU.add)
                call("pool", "tensor_tensor", b_[:, :], a_[:, :], v_[:, :], ALU.mult)
                call("act", "activation", a_[:, :], b_[:, :], ACT.Sigmoid, scale=2.0 * GC)
                call("pool", "tensor_tensor", z_[:, :], a_[:, :], v_[:, :], ALU.mult)
                gtok = b * S + n * 128
                dd, off = gtok // TB, gtok % TB
                ex_events.append(dma("sp", EXin[dd, 0:128, off:off + 128], z_[:, :], z_))
            Sx.emit()
            esb.close()

            esb = ExitStack()
            Sx.es_cur = esb
            NW = 3
            zps = Rot([ps([128, 512], F32, "z") for _ in range(2)])
            accs = [ps([128, 512], F32, "acc") for _ in range(NW)]
            ops_ = [ps([128, 512], F32, "o") for _ in range(NW)]
            es_ = Rot([sb([128, 512], F32, "e") for _ in range(2 * NW)])
            sps = Rot([sb([128, 512], BF16, "sp") for _ in range(2 * NW)])
            ecs = Rot([sb([128, 512], F32, "ecs") for _ in range(NW)])
            ats = Rot([sb([128, 512], BF16, "at") for _ in range(NW)])
            osb = Rot([sb([128, 512], BF16, "osb") for _ in range(2)])
            if b == 0:
                convert_weights()

            def z_mm(G, blk):
                kb, KS, c0, diag = blk
                q0 = G * 512
                kc0 = 0 if kb < 0 else 16 + kb * 128
                z = zps.next()
                call("pe", "matmul", z[0:KS, c0:512], kT[:, kc0:kc0 + KS], qT[:, q0 + c0:q0 + 512],
                     start=True, stop=True)
                return z

            def esp(z, blk):
                kb, KS, c0, diag = blk
                e_ = es_.next()
                sp = sps.next()
                call("act", "activation", e_[0:KS, c0:512], z[0:KS, c0:512], ACT.Exp, scale=SCALE)
                call("act", "activation", sp[0:KS, c0:512], e_[0:KS, c0:512], ACT.Ln, bias=1.0)
                if diag:
                    call("dve", "tensor_tensor", sp[:, c0:c0 + 128], sp[:, c0:c0 + 128], Mdiag, ALU.mult)
                return (e_, sp)

            for w0 in range(0, NG, NW):
                wave = list(range(w0, min(NG, w0 + NW)))
                blks = {}
                accd, od = {}, {}
                for wi, G in enumerate(wave):
                    bl = []
                    for jj in (3, 2, 1, 0):
                        bl.append((4 * G + jj, 128, 128 * jj, True))
                    for kb in range(4 * G - 1, -1, -1):
                        bl.append((kb, 128, 0, False))
                    bl.append((-1, 16, 0, False))
                    blks[G] = bl
                    accd[G], od[G] = accs[wi], ops_[wi]
                    call("pe", "matmul", accs[wi][:, :], zero_b[:, 0:128], zero_b[:, :], start=True, stop=True)
                    call("pe", "matmul", ops_[wi][:, :], zero_b[:, 0:128], zero_b[:, :], start=True, stop=True)
                st = {}
                for G in wave:
                    st[G] = esp(z_mm(G, blks[G][0]), blks[G][0])
                maxlen = max(len(blks[G]) for G in wave)
                for i in range(maxlen):
                    act_g = [G for G in wave if i < len(blks[G])]
                    nxt_g = [G for G in wave if i + 1 < len(blks[G])]
                    zt = {}
                    for G in nxt_g[:2]:
                        zt[G] = z_mm(G, blks[G][i + 1])
                    for G in act_g:
                        kb, KS, c0, diag = blks[G][i]
                        call("pe", "matmul", accd[G][0:KS, c0:512], Umat(KS), st[G][1][0:KS, c0:512],
                             start=False, stop=True, skip_group_check=True)
                    ecd = {}
                    for G in act_g:
                        kb, KS, c0, diag = blks[G][i]
                        ec = ecs.next()
                        call("act", "activation", ec[0:KS, c0:512], accd[G][0:KS, c0:512], ACT.Exp, scale=-1.0)
                        ecd[G] = ec
                    for G in act_g:
                        kb, KS, c0, diag = blks[G][i]
                        if i < len(blks[G]) - 1:
                            call("pe", "matmul", accd[G][0:KS, c0:512], SLmat(KS), st[G][1][0:KS, c0:512],
                                 start=False, stop=True, skip_group_check=True)
                    nxt = {}
                    for G in nxt_g[:2]:
                        nxt[G] = esp(zt[G], blks[G][i + 1])
                    for G in nxt_g[2:]:
                        zt[G] = z_mm(G, blks[G][i + 1])
                        nxt[G] = esp(zt[G], blks[G][i + 1])
                    atd = {}
                    for G in act_g:
                        kb, KS, c0, diag = blks[G][i]
                        at = ats.next()
                        call("dve", "tensor_tensor", at[0:KS, c0:512], st[G][0][0:KS, c0:512], ecd[G][0:KS, c0:512], ALU.mult)
                        if diag:
                            call("dve", "tensor_tensor", at[:, c0:c0 + 128], at[:, c0:c0 + 128], Mdiag, ALU.mult)
                        atd[G] = at
                    for G in act_g:
                        kb, KS, c0, diag = blks[G][i]
                        vb = 0 if kb < 0 else 1 + kb
                        call("pe", "matmul", od[G][:, c0:512], Vt[0:KS, vb, :], atd[G][0:KS, c0:512],
                             start=False, stop=True, skip_group_check=True)
                    st = nxt
                for wi, G in enumerate(wave):
                    ob = osb.next()
                    call("act", "activation", ob[:, :], ops_[wi][:, :], ACT.Copy)
                    gtok = b * S + G * 512
                    dd, off = gtok // TB, gtok % TB
                    ex_events.append(dma("sp", EXin[dd, 128:256, off:off + 512], ob[:, :], ob))
            Sx.emit()
            esb.close()

        es_a.close()

        Sx.wait_all("pool", ex_events)

        def cc(e):
            return e.collective_compute("AllGather", ALU.bypass, replica_groups=[list(range(8))],
                                        ins=[EXin_t.ap().rearrange("d f t -> (d f) t").opt()],
                                        outs=[EXout_t.ap().rearrange("s d f t -> (s d f) t").opt()])
        Sx.prog["pool"].append(([], cc, (ccs, 1)))
        EXout.w = (ccs, 1, "cc")

        esb = ExitStack()
        Sx.es_cur = esb
        bc = sb([128, 48], F32, "bc")
        dma("sp", bc[:, :], bcols[:, :], bc)
        lnr = sb([128, 2, 2048], F32, "lnr")
        dma("sp", lnr[:, :, :], lnrep[:, :, :], lnr)
        xb = sb([128, NKC, 512], BF16, "xb")
        zT = sb([128, 8, 512], BF16, "zT")
        oT = sb([128, 8, 512], BF16, "oT")
        ysT = sb([128, 8, 512], BF16, "ysT")
        ybT = sb([128, 8, 512], BF16, "ybT")
        mxT = sb([128, NKC, 512], BF16, "mxT")
        rr = [sb([128, 2048], F32, f"rr{i}") for i in range(4)]
        w16 = Rot([sb([128, NKC, 128], BF16, "w16") for _ in range(5)])
        w8 = Rot([sb([128, 8, 128], BF16, "w8") for _ in range(6)])
        wo = Rot([sb([128, NKC, 512], BF16, "wo") for _ in range(2)])
        pB = Rot([ps([128, 512], F32, "pB") for _ in range(6)])
        sA = Rot([sb([128, 512], F32, "sA") for _ in range(2)])
        sBt = Rot([sb([128, 512], F32, "sB") for _ in range(2)])
        xtk = Rot([sb([128, 512], F32, "xtk") for _ in range(3)])
        st6 = sb([128, 4, 6], F32, "st6")
        mv = sb([128, 2], F32, "mv")
        rs = sb([128, 2], F32, "rs")
        ALPHA = 2.0 ** 0.25
        out_events = []
        oh = sb([128, 8], F32, "oh")
        dma("sp", oh[:, :], onehot[:, :], oh)
        stg = Rot([sb([128, 8, 512], BF16, "stg") for _ in range(1)])
        accf = Rot([sb([128, 512], F32, "accf") for _ in range(2)])

        wq = Rot(["sp"])

        def wload(wt, src, nk, m):
            wbuf = Wb[src]
            dma(wq.next(), wt[:, 0:nk, :], wbuf.v(wbuf.t[m, :, :].rearrange("p (kc c) -> p kc c", kc=nk)), wt)

        def lin(pp, wt, rhs, nk):
            for kc in range(nk):
                call("pe", "matmul", pp[:, :], wt[:, kc, :], rhs[:, kc, :], start=(kc == 0), stop=(kc == nk - 1))

        for ch in range(NCH):
            t0 = ch * 512
            dma("pool", xb[:, :, :], xTB[:, t0:t0 + 512].rearrange("(kc p) t -> p kc t", p=128), xb)
            for s_ in range(8):
                for (dst, f0) in ((zT, 0), (oT, 128)):
                    sg = stg.next()
                    dma("sp", sg[:, :, :], EXout.v(EXout_t.ap()[s_, :, f0:f0 + 128, t0:t0 + 512].rearrange("d p t -> p d t")), sg)
                    af = accf.next()
                    call("dve", "tensor_scalar", af[:, :], sg[:, 0, :], oh[:, 0:1], None, ALU.mult)
                    for d_ in range(1, 8):
                        dstv = dst[:, s_, :] if d_ == 7 else af[:, :]
                        call("dve", "scalar_tensor_tensor", dstv, sg[:, d_, :], oh[:, d_:d_ + 1], af[:, :], ALU.mult, ALU.add)
            for m in range(8):
                wa = w8.next(); wload(wa, "w_glu", 8, m)
                wb_ = w8.next(); wload(wb_, "w_glu", 8, 8 + m)
                wg = w16.next(); wload(wg, "w_ing", NKC, m)
                pa = pB.next(); lin(pa, wa, zT, 8)
                pb = pB.next(); lin(pb, wb_, zT, 8)
                pg = pB.next(); lin(pg, wg, xb, NKC)
                a1 = sA.next(); a2 = sBt.next()
                call("act", "activation", a1[:, :], pb[:, :], ACT.Sigmoid, bias=bc[:, 8 + m:9 + m])
                call("act", "activation", a2[:, :], pg[:, :], ACT.Silu)
                call("dve", "scalar_tensor_tensor", a1[:, :], pa[:, :], bc[:, m:m + 1], a1[:, :], ALU.add, ALU.mult)
                call("pool", "tensor_tensor", ysT[:, m, :], a1[:, :], a2[:, :], ALU.mult)
            for m in range(8):
                wg = w16.next(); wload(wg, "w_ing", NKC, 8 + m)
                pg = pB.next(); lin(pg, wg, xb, NKC)
                a2 = sBt.next()
                call("act", "activation", a2[:, :], pg[:, :], ACT.Silu)
                call("pool", "tensor_tensor", ybT[:, m, :], a2[:, :], oT[:, m, :], ALU.mult)
            for m in range(NKC):
                ws_ = w8.next(); wload(ws_, "w_bs", 8, m)
                wb_ = w8.next(); wload(wb_, "w_bb", 8, m)
                wg1 = w16.next(); wload(wg1, "w_gate", NKC, m)
                wg2 = w16.next(); wload(wg2, "w_gate", NKC, 16 + m)
                p1 = pB.next(); lin(p1, ws_, ysT, 8)
                p2 = pB.next(); lin(p2, wb_, ybT, 8)
                p3 = pB.next(); lin(p3, wg1, xb, NKC)
                p4 = pB.next(); lin(p4, wg2, xb, NKC)
                a1 = sA.next(); a2 = sBt.next()
                call("act", "activation", a1[:, :], p3[:, :], ACT.Sigmoid, bias=bc[:, 16 + m:17 + m])
                call("act", "activation", a2[:, :], p4[:, :], ACT.Sigmoid, bias=bc[:, 32 + m:33 + m])
                call("dve", "tensor_tensor", a1[:, :], a1[:, :], p1[:, :], ALU.mult)
                call("dve", "tensor_tensor", a2[:, :], a2[:, :], p2[:, :], ALU.mult)
                call("pool", "tensor_tensor", mxT[:, m, :], a1[:, :], a2[:, :], ALU.add)
            for sl in range(4):
                wt = wo.next()
                dma(wq.next(), wt[:, :, :], Wb["w_out"].v(Wb["w_out"].t[sl, :, :].rearrange("p (kc c) -> p kc c", kc=NKC)), wt)
                for t in range(4):
                    pp = pB.next()
                    for kc in range(NKC):
                        call("pe", "matmul", pp[:, :], mxT[:, kc, t * 128:(t + 1) * 128], wt[:, kc, :],
                             start=(kc == 0), stop=(kc == NKC - 1))
                    xk = xtk.next()
                    dma("sp", xk[:, :], xtok[t0 + t * 128:t0 + (t + 1) * 128, sl * 512:(sl + 1) * 512], xk)
                    call("dve", "scalar_tensor_tensor", rr[t][:, sl * 512:(sl + 1) * 512], xk[:, :], ALPHA, pp[:, :],
                         ALU.mult, ALU.add)
            for t in range(4):
                r = rr[t]
                for sl in range(4):
                    call("dve", "bn_stats", st6[:, sl, :], r[:, sl * 512:(sl + 1) * 512])
                call("dve", "bn_aggr", mv[:, :], st6[:, :, :].ap if False else st6.v(st6.t[:, :, :].rearrange("p a b -> p (a b)")))
                call("dve", "tensor_scalar", rs[:, 0:1], mv[:, 1:2], 1e-5, None, ALU.add)
                call("act", "activation", rs[:, 0:1], rs[:, 0:1], ACT.Sqrt)
                call("dve", "reciprocal", rs[:, 0:1], rs[:, 0:1])
                call("dve", "scalar_tensor_tensor", rs[:, 1:2], mv[:, 0:1], -1.0, rs[:, 0:1], ALU.mult, ALU.mult)
                call("act", "activation", r[:, :], r[:, :], ACT.Identity, bias=rs[:, 1:2], scale=rs[:, 0:1])
                call("dve", "tensor_tensor", r[:, :], r[:, :], lnr[:, 0, :], ALU.mult)
                call("pool", "tensor_tensor", r[:, :], r[:, :], lnr[:, 1, :], ALU.add)
                out_events.append(dma("sp", out[t0 + t * 128:t0 + (t + 1) * 128, :], r[:, :], r))
        Sx.wait_all("sp", out_events)
        Sx.emit()
        esb.close()
    return nc


def _prep(inputs, S):
    f = np.float32
    x = np.asarray(inputs["x"], f)
    TB = S // 4
    xTall = np.ascontiguousarray(np.transpose(x, (0, 2, 1)))
    metaT = np.ascontiguousarray(np.asarray(inputs["meta_tokens"], f).T)
    w_in = np.asarray(inputs["w_in"], f)[0]
    lre = np.asarray(inputs["ssm_lambda_re"], f)[0]
    lim = np.asarray(inputs["ssm_lambda_im"], f)[0]
    ldt = np.asarray(inputs["ssm_log_dt"], f)[0]
    bre = np.asarray(inputs["ssm_b_re"], f)[0]
    bim = np.asarray(inputs["ssm_b_im"], f)[0]
    cre = np.asarray(inputs["ssm_c_re"], f)[0]
    cim = np.asarray(inputs["ssm_c_im"], f)[0]
    dsk = np.asarray(inputs["ssm_d"], f)[0]
    cst = np.zeros((128, 650), f)
    cst[:, 0] = np.arange(1, 129)
    cst[:, 1] = -np.arange(1, 129)
    cst[:, 2:130] = np.arange(1, 129)[None, :]
    for g8 in range(8):
        cst[16 * g8:16 * g8 + 16, 130 + g8] = 1.0
    ii = np.arange(128)
    cst[:, 138:266] = (ii[:, None] <= ii[None, :])
    cst[:, 266:394] = (ii[:, None] >= ii[None, :])
    cst[:, 394:522] = (ii[:, None] < ii[None, :])
    cst[:, 522:650] = (ii[:, None] < ii[None, :])
    w_ing = np.ascontiguousarray(np.concatenate([w_in[:, 1024:2048], w_in[:, 5120:6144]], axis=1))
    bg = np.asarray(inputs["b_glu"], f)[0]
    bgate = np.asarray(inputs["b_gate"], f)[0]
    bcols = np.ascontiguousarray(np.concatenate([bg.reshape(16, 128).T, bgate.reshape(32, 128).T], axis=1))
    lnrep = np.ascontiguousarray(np.stack([
        np.broadcast_to(np.asarray(inputs["ln_gain"], f)[0][None, :], (128, 2048)),
        np.broadcast_to(np.asarray(inputs["ln_bias"], f)[0][None, :], (128, 2048))], axis=1))
    shared = dict(xT=xTall, metaT=metaT, cst=cst, w_ing=w_ing,
                  w_glu=np.asarray(inputs["w_glu"], f)[0], w_bs=np.asarray(inputs["w_branch_ssm"], f)[0],
                  w_bb=np.asarray(inputs["w_branch_sb"], f)[0], w_gate=np.asarray(inputs["w_gate"], f)[0],
                  w_out=np.asarray(inputs["w_out"], f)[0], bcols=bcols, lnrep=lnrep)
    maps = []
    for c in range(8):
        gs = slice(8 * c, 8 * c + 8)
        cols = np.concatenate([np.arange(128 * c, 128 * c + 128), 2048 + np.arange(128 * c, 128 * c + 128),
                               3072 + np.arange(128 * c, 128 * c + 128), 4096 + np.arange(128 * c, 128 * c + 128)])
        wA = np.ascontiguousarray(w_in[:, cols])
        lr8, li8, ld8 = lre[gs], lim[gs], ldt[gs]
        ld8p = np.repeat(ld8[:, None], 64, axis=1)
        rows = np.stack([lr8.reshape(512), li8.reshape(512), ld8p.reshape(512)], axis=0)
        lam_rows = np.ascontiguousarray(np.broadcast_to(rows[None], (128, 3, 512)))
        def colify(a):
            return a.reshape(4, 2, 64).transpose(1, 2, 0).reshape(128, 4)
        lam_cols = np.ascontiguousarray(np.stack([colify(lr8), colify(li8), colify(ld8p)], axis=1))
        lamB = np.ascontiguousarray(np.stack([np.repeat(lr8, 16, axis=0), np.repeat(li8, 16, axis=0),
                                              np.repeat(ld8p, 16, axis=0)], axis=1))
        BTc = np.ascontiguousarray(np.stack([bre[gs].transpose(0, 2, 1).reshape(128, 64),
                                             bim[gs].transpose(0, 2, 1).reshape(128, 64)], axis=1))
        Cp = np.zeros((128, 2, 4, 128), f)
        for pr in range(4):
            for g2 in range(2):
                g8 = 2 * pr + g2
                Cp[64 * g2:64 * g2 + 64, 0, pr, 16 * g8:16 * g8 + 16] = cre[8 * c + g8].T
                Cp[64 * g2:64 * g2 + 64, 1, pr, 16 * g8:16 * g8 + 16] = cim[8 * c + g8].T
        dcol = np.ascontiguousarray(dsk[gs].reshape(128, 1))
        b_, j_ = c // 4, c % 4
        xTB = np.ascontiguousarray(xTall[b_][:, j_ * TB:(j_ + 1) * TB])
        xtok = np.ascontiguousarray(x[b_, j_ * TB:(j_ + 1) * TB, :])
        m = dict(shared)
        m.update(wA=wA, lam_rows=lam_rows, lam_cols=lam_cols, lamB=lamB, BT=BTc, Cpad=Cp, dcol=dcol,
                 xTB=xTB, xtok=xtok, onehot=np.ascontiguousarray(np.broadcast_to(np.eye(8, dtype=f)[c][None, :], (128, 8))))
        maps.append(m)
    return maps


def kernel(**inputs):
    x = np.asarray(inputs["x"])
    S = x.shape[1]
    TB = S // 4
    nc = build(S)
    maps = _prep(inputs, S)
    res = run_bass_kernel_spmd(nc, maps, core_ids=list(range(8)))
    out = np.zeros((2, S, D), np.float32)
    for c in range(8):
        out[c // 4, (c % 4) * TB:(c % 4 + 1) * TB, :] = res.results[c]["out"]
    return out
```
